# Optimizing a Trainium2 kernel written in Bass

```python
import jax, jax.numpy as jnp
from jax import lax
import numpy as np

D_MODEL = 1024
BATCH = 16
SEQ = 256
DEPTH = 2
DEC_BATCH = 4
DEC_SEQ = 2048
PAST_LEN = 512

GRID_W = 64
N_MIXERS = 2
N_RWKV = (DEPTH + 1) // 2
N_SGU = DEPTH // 2
RW_HEAD_DIM = 64
RW_HEADS = D_MODEL // RW_HEAD_DIM
DECAY_LORA = 64
AAA_LORA = 64
GATE_LORA = 128
N_SHIFT = 6
SGU_WIDTH = 2 * D_MODEL
SGU_GROUPS = 8
CHUNK = 128
FFN_HIDDEN = 2816
CONV_W = 3
N_MOD = 6
RMS_EPS = 1e-6
LN_EPS = 1e-5
GN_EPS = 64e-5

kernel_name = "hybrid_rwkv7_sgu_convffn_diffusion_step"


def rmsnorm(x, g):
    xf = x.astype(jnp.float32)
    y = xf * lax.rsqrt(jnp.mean(xf * xf, axis=-1, keepdims=True) + RMS_EPS)
    return (y * g.astype(jnp.float32)).astype(x.dtype)


def layernorm(x, g, b):
    xf = x.astype(jnp.float32)
    mu = jnp.mean(xf, axis=-1, keepdims=True)
    var = jnp.mean(jnp.square(xf - mu), axis=-1, keepdims=True)
    y = (xf - mu) * lax.rsqrt(var + LN_EPS)
    return (y * g.astype(jnp.float32) + b.astype(jnp.float32)).astype(x.dtype)


def shift_seq(x):
    prev = jnp.pad(x[:, :-1], ((0, 0), (1, 0), (0, 0)))
    nxt = jnp.pad(x[:, 1:], ((0, 0), (0, 1), (0, 0)))
    return prev, nxt


def wkv_scan(r, w, k, v, aa, bb, s0, reverse):
    xs = tuple(jnp.moveaxis(t, 1, 0) for t in (r, w, k, v, aa, bb))

    def step(S, inp):
        r_t, w_t, k_t, v_t, a_t, b_t = inp
        sa = jnp.einsum('bhij,bhj->bhi', S, a_t)
        S = S * w_t[:, :, None, :] + sa[..., None] * b_t[:, :, None, :] + v_t[..., None] * k_t[:, :, None, :]
        y = jnp.einsum('bhij,bhj->bhi', S, r_t)
        return S, y

    S, ys = lax.scan(step, s0, xs, reverse=reverse)
    return jnp.moveaxis(ys, 0, 1), S


def rwkv7_mix(h, s0_fwd, s0_bwd, mu, w_r, w_k, w_v, w_o, w0, w1, w2, a0, a1, a2, g1, g2,
              k_k, k_a, r_k, lnx_w, lnx_b):
    B, T, D = h.shape
    H, K = RW_HEADS, RW_HEAD_DIM
    f32 = jnp.float32
    heads = lambda t: t.reshape(B, T, H, K).astype(f32)
    prev, nxt = shift_seq(h)
    xs = h[:, :, None, :] + (prev - h)[:, :, None, :] * mu[0] + (nxt - h)[:, :, None, :] * mu[1]
    xr, xw, xk, xv, xa, xg = [xs[:, :, i] for i in range(N_SHIFT)]
    r = xr @ w_r
    k = xk @ w_k
    v = xv @ w_v
    g = jax.nn.sigmoid(xg @ g1) @ g2
    rh, kh, vh = heads(r), heads(k), heads(v)
    kk = kh * k_k.reshape(H, K).astype(f32)
    kk = kk / jnp.maximum(jnp.sqrt(jnp.sum(kk * kk, axis=-1, keepdims=True)), 1e-12)
    y = jnp.zeros_like(rh)
    k_bonus = jnp.zeros_like(kh)
    finals = []
    for d, (s0, rev) in enumerate(((s0_fwd, False), (s0_bwd, True))):
        w_raw = (w0[d] + jnp.tanh(xw @ w1[d]) @ w2[d]).astype(f32)
        decay = jnp.exp(-jnp.exp(-jax.nn.softplus(-w_raw) - 0.5))
        a = heads(jax.nn.sigmoid((a0[d] + (xa @ a1[d]) @ a2[d]).astype(f32)))
        kd = kh * (1.0 + (a - 1.0) * k_a.reshape(H, K).astype(f32))
        yd, sd = wkv_scan(rh, heads(decay), kd, vh, -kk, kk * a, s0.astype(f32), rev)
        y = y + yd
        k_bonus = k_bonus + kd
        finals.append(sd.astype(h.dtype))
    mean = jnp.mean(y, axis=-1, keepdims=True)
    var = jnp.mean(jnp.square(y - mean), axis=-1, keepdims=True)
    yn = (y - mean) * lax.rsqrt(var + GN_EPS)
    yn = yn * lnx_w.reshape(H, K).astype(f32) + lnx_b.reshape(H, K).astype(f32)
    yn = yn + jnp.sum(rh * k_bonus * r_k.astype(f32), axis=-1, keepdims=True) * vh
    out = (yn.reshape(B, T, D).astype(h.dtype) * g) @ w_o
    return out, finals[0], finals[1]


def sgu_mix(h, w_in, ln_w, ln_b, w_s, b_s, w_out):
    B, T, _ = h.shape
    z = jax.nn.gelu(h @ w_in, approximate=False)
    u, v = jnp.split(z, 2, axis=-1)
    v = layernorm(v, ln_w, ln_b)
    vc = v.reshape(B, T // CHUNK, CHUNK, SGU_GROUPS, SGU_WIDTH // SGU_GROUPS)
    vm = jnp.einsum('gpq,bnqgc->bnpgc', w_s, vc) + b_s.T[:, :, None]
    return (u * vm.reshape(B, T, SGU_WIDTH)) @ w_out


def conv_ffn(h, w_up, w_conv, b_conv, w_down, on_grid):
    B, T, _ = h.shape
    up = h @ w_up
    C = up.shape[-1]
    if on_grid:
        rows = T // GRID_W
        img = up.reshape(B, rows, GRID_W, C)
        img = lax.conv_general_dilated(img, w_conv[:, :, None, :], (1, 1), 'SAME',
                                       dimension_numbers=('NHWC', 'HWIO', 'NHWC'),
                                       feature_group_count=C)
        up = img.reshape(B, T, C)
    else:
        prev, nxt = shift_seq(up)
        up = prev * w_conv[1, 0] + up * w_conv[1, 1] + nxt * w_conv[1, 2]
    up = up + b_conv
    val, gate = jnp.split(up, 2, axis=-1)
    return (jax.nn.silu(gate) * val) @ w_down


def setup_inputs(seed: int = 0) -> dict:
    key = jax.random.key(seed)
    keys = iter(jax.random.split(key, 48))
    D, H, K, F, E, G = D_MODEL, RW_HEADS, RW_HEAD_DIM, FFN_HIDDEN, SGU_WIDTH, SGU_GROUPS

    def nrm(shape, scale):
        return jax.random.normal(next(keys), shape, jnp.float32) * scale

    def gain(shape):
        return 1.0 + nrm(shape, 0.05)

    return {
        "x_prompt": nrm((BATCH, SEQ, D), 1.0),
        "x_sample": nrm((DEC_BATCH, DEC_SEQ, D), 1.0),
        "state_ctx_fwd": nrm((DEC_BATCH, N_RWKV, H, K, K), 0.5),
        "state_ctx_bwd": nrm((DEC_BATCH, N_RWKV, H, K, K), 0.5),
        "c": nrm((DEC_BATCH, D), 1.0),
        "c_ctx": nrm((D,), 1.0),
        "ada_w": nrm((DEPTH, D, N_MOD * D), 0.5 * D ** -0.5),
        "ada_b": nrm((DEPTH, N_MOD * D), 0.02),
        "norm_mix": gain((DEPTH, D)),
        "norm_ffn": gain((DEPTH, D)),
        "ffn_up": nrm((DEPTH, D, 2 * F), D ** -0.5),
        "ffn_conv": nrm((DEPTH, CONV_W, CONV_W, 2 * F), 1.0 / 3.0),
        "ffn_conv_b": nrm((DEPTH, 2 * F), 0.02),
        "ffn_down": nrm((DEPTH, F, D), F ** -0.5),
        "norm_final": gain((D,)),
        "rw_mu": jax.random.uniform(next(keys), (N_RWKV, 2, N_SHIFT, D), jnp.float32, 0.0, 0.5),
        "rw_wr": nrm((N_RWKV, D, D), D ** -0.5),
        "rw_wk": nrm((N_RWKV, D, D), D ** -0.5),
        "rw_wv": nrm((N_RWKV, D, D), D ** -0.5),
        "rw_wo": nrm((N_RWKV, D, D), D ** -0.5),
        "rw_w0": nrm((N_RWKV, 2, D), 0.5),
        "rw_w1": nrm((N_RWKV, 2, D, DECAY_LORA), D ** -0.5),
        "rw_w2": nrm((N_RWKV, 2, DECAY_LORA, D), 0.1 * DECAY_LORA ** -0.5),
        "rw_a0": nrm((N_RWKV, 2, D), 0.5),
        "rw_a1": nrm((N_RWKV, 2, D, AAA_LORA), D ** -0.5),
        "rw_a2": nrm((N_RWKV, 2, AAA_LORA, D), 0.1 * AAA_LORA ** -0.5),
        "rw_g1": nrm((N_RWKV, D, GATE_LORA), D ** -0.5),
        "rw_g2": nrm((N_RWKV, GATE_LORA, D), GATE_LORA ** -0.5),
        "rw_kk": 0.85 + nrm((N_RWKV, D), 0.05),
        "rw_ka": gain((N_RWKV, D)),
        "rw_rk": nrm((N_RWKV, H, K), 0.1),
        "rw_lnx_w": gain((N_RWKV, D)),
        "rw_lnx_b": nrm((N_RWKV, D), 0.02),
        "sg_in": nrm((N_SGU, D, 2 * E), D ** -0.5),
        "sg_ln_w": gain((N_SGU, E)),
        "sg_ln_b": nrm((N_SGU, E), 0.02),
        "sg_ws": nrm((N_SGU, G, CHUNK, CHUNK), CHUNK ** -0.5),
        "sg_bs": gain((N_SGU, G, CHUNK)),
        "sg_out": nrm((N_SGU, E, D), E ** -0.5),
    }


def reference(x_prompt, x_sample, state_ctx_fwd, state_ctx_bwd, c, c_ctx,
              ada_w, ada_b, norm_mix, norm_ffn, ffn_up, ffn_conv, ffn_conv_b, ffn_down, norm_final,
              rw_mu, rw_wr, rw_wk, rw_wv, rw_wo, rw_w0, rw_w1, rw_w2, rw_a0, rw_a1, rw_a2,
              rw_g1, rw_g2, rw_kk, rw_ka, rw_rk, rw_lnx_w, rw_lnx_b,
              sg_in, sg_ln_w, sg_ln_b, sg_ws, sg_bs, sg_out):

    def run_stream(x, cond, s0_fwd, s0_bwd, on_grid):
        new_f, new_b = [], []
        sc = jax.nn.silu(cond)
        for i in range(DEPTH):
            mod = (sc @ ada_w[i] + ada_b[i])[:, None, :]
            sh1, sc1, gt1, sh2, sc2, gt2 = jnp.split(mod, N_MOD, axis=-1)
            h = rmsnorm(x, norm_mix[i]) * (1.0 + sc1) + sh1
            j = i // N_MIXERS
            if i % N_MIXERS == 0:
                out, sf, sb = rwkv7_mix(h, s0_fwd[:, j], s0_bwd[:, j], rw_mu[j], rw_wr[j], rw_wk[j], rw_wv[j],
                                        rw_wo[j], rw_w0[j], rw_w1[j], rw_w2[j], rw_a0[j], rw_a1[j], rw_a2[j],
                                        rw_g1[j], rw_g2[j], rw_kk[j], rw_ka[j], rw_rk[j], rw_lnx_w[j], rw_lnx_b[j])
                new_f.append(sf)
                new_b.append(sb)
            else:
                out = sgu_mix(h, sg_in[j], sg_ln_w[j], sg_ln_b[j], sg_ws[j], sg_bs[j], sg_out[j])
            x = x + gt1 * out
            h = rmsnorm(x, norm_ffn[i]) * (1.0 + sc2) + sh2
            x = x + gt2 * conv_ffn(h, ffn_up[i], ffn_conv[i], ffn_conv_b[i], ffn_down[i], on_grid)
        return rmsnorm(x, norm_final), jnp.stack(new_f, axis=1), jnp.stack(new_b, axis=1)

    zero_state = jnp.zeros((x_prompt.shape[0], N_RWKV, RW_HEADS, RW_HEAD_DIM, RW_HEAD_DIM), x_prompt.dtype)
    y_prompt, new_state_fwd, new_state_bwd = run_stream(x_prompt, c_ctx[None, :], zero_state, zero_state, False)

    y_sample, _, _ = run_stream(x_sample, c, state_ctx_fwd, state_ctx_bwd, True)

    return (y_prompt, y_sample, new_state_fwd, new_state_bwd)
```

```python
import numpy as np
from contextlib import ExitStack
import concourse.bass as bass
import concourse.mybir as mybir
from concourse.bass_utils import run_bass_kernel_spmd

F32 = mybir.dt.float32
BF16 = mybir.dt.bfloat16
AF = mybir.ActivationFunctionType
ALU = mybir.AluOpType

D = 1024
NCH = 8
NHP = 8
LP = 256
LS = 2048
WIN = 1280
OUTS = 1024
TT = 2 * LP + WIN
XB_P = [0, LP]
XB_S = 2 * LP
HB_P = [1, LP + 3]
HB_S = 2 * (LP + 2) + 1
HL = 2 * (LP + 2) + LS + 2
F = 2816
NFC = 22
E2 = 2048
CDEC = 0.6065306597126334
RMS_EPS = 1e-6
LN_EPS = 1e-5
GN_EPS = 64e-5
NB = 256

ENGS = ["pe", "act", "dve", "pool", "sp"]
SAME_SYNC = {"pe": False, "act": True, "dve": True, "pool": True, "sp": False}


class Buf:
    __slots__ = ("name", "lw", "rd", "dsem", "dcnt", "excl", "pe_rows")

    def __init__(self, name, excl=False):
        self.name = name
        self.lw = None
        self.rd = {}
        self.dsem = None
        self.dcnt = 0
        self.excl = excl
        self.pe_rows = None


class Prog:
    def __init__(self, nc):
        self.nc = nc
        self.ops = {e: [] for e in ENGS}
        self.cnt = {e: 0 for e in ENGS}
        self.waited = {e: {} for e in ENGS}
        self.sem_objs = {}
        self.dma_keys = []
        self.bulk = {}
        self.sembufs = []
        self.inflight = {"sp": [], "pool": []}
        self.max_inflight = {"sp": 8, "pool": 2}

    def _need(self, eng, deps, key, val):
        if key == eng and not SAME_SYNC[eng]:
            return
        if self.waited[eng].get(key, 0) >= val:
            return
        if deps.get(key, 0) < val:
            deps[key] = val

    def _collect(self, eng, reads, writes):
        deps = {}
        for b in reads:
            if b.lw is not None:
                self._need(eng, deps, *b.lw)
        for b in writes:
            if b.lw is not None:
                self._need(eng, deps, *b.lw)
            for k, v in b.rd.items():
                self._need(eng, deps, k, v)
        for k, v in deps.items():
            self.waited[eng][k] = v
        return list(deps.items())

    def op(self, eng, fn, reads=(), writes=(), rows=None):
        ex = [b for b in reads if b.excl]
        if ex:
            reads = [b for b in reads if not b.excl]
            writes = list(writes) + ex
        waits = self._collect(eng, reads, writes)
        if eng == "pe" and rows is not None:
            for b in writes:
                if b.excl and b.pe_rows is not None:
                    r0, n0, sq = b.pe_rows
                    if (r0 + n0 <= rows[0] or rows[0] + rows[1] <= r0) and self.waited["pe"].get("pe", 0) < sq:
                        self.waited["pe"]["pe"] = sq
                        waits = waits + [("pe", sq)]
        self.cnt[eng] += 1
        seq = self.cnt[eng]
        if eng == "pe" and rows is not None:
            for b in writes:
                if b.excl:
                    b.pe_rows = (rows[0], rows[1], seq)
        self.ops[eng].append((waits, fn, (eng, 1)))
        for b in reads:
            b.rd[eng] = seq
        for b in writes:
            b.lw = (eng, seq)
            b.rd = {}
        return seq

    def dma(self, q, out_ap, in_ap, reads=(), writes=(), sembuf=None, bulk=None):
        waits = self._collect(q, reads, writes)
        if bulk is not None:
            key = ("bulk", bulk, q)
            self.bulk[key] = self.bulk.get(key, 0) + 16
            val = self.bulk[key]
        else:
            if sembuf.dsem is None:
                sembuf.dsem = {}
                sembuf.dcnt = {}
                self.sembufs.append(sembuf)
            if q not in sembuf.dsem:
                sembuf.dsem[q] = ("dma", len(self.dma_keys), q)
                sembuf.dcnt[q] = 0
                self.dma_keys.append(sembuf.dsem[q])
            sembuf.dcnt[q] += 16
            key = sembuf.dsem[q]
            val = sembuf.dcnt[q]

        def fn(e, out_ap=out_ap, in_ap=in_ap):
            return e.dma_start(out=out_ap, in_=in_ap)
        fl = self.inflight[q]
        while len(fl) >= self.max_inflight[q]:
            k0, v0 = fl.pop(0)
            if self.waited[q].get(k0, 0) < v0:
                self.waited[q][k0] = v0
                waits = waits + [(k0, v0)]
        fl.append((key, val))
        self.ops[q].append((waits, fn, (key, 16)))
        for b in reads:
            b.rd[key] = max(b.rd.get(key, 0), val)
        for b in writes:
            b.lw = (key, val)
            b.rd = {}

    def barrier(self):
        for e in ENGS:
            deps = {}
            for f in ENGS:
                if f != e and self.cnt[f] > 0:
                    self._need(e, deps, f, self.cnt[f])
            for k, v in self.bulk.items():
                self._need(e, deps, k, v)
            for b in self.sembufs:
                for q, k in b.dsem.items():
                    self._need(e, deps, k, b.dcnt[q])
            for k, v in deps.items():
                self.waited[e][k] = v
            if deps:
                self.ops[e].append((list(deps.items()), None, None))

    def final_wait(self, eng="sp"):
        waits = [(k, v) for k, v in self.bulk.items()]
        for b in self.sembufs:
            for q, k in b.dsem.items():
                waits.append((k, b.dcnt[q]))
        self.ops[eng].append((waits, None, None))

    def emit(self, ctx):
        nc = self.nc
        keys = list(ENGS) + self.dma_keys + list(self.bulk.keys())
        for k in keys:
            nm = "s_" + "_".join(str(x) for x in (k if isinstance(k, tuple) else (k,)))
            self.sem_objs[k] = ctx.enter_context(nc.semaphore(nm))
        block = ctx.enter_context(nc.Block())

        def run(e, name):
            for waits, fn, inc in self.ops[name]:
                for k, v in waits:
                    e.wait_ge(self.sem_objs[k], v)
                if fn is not None:
                    ins = fn(e)
                    ins.then_inc(self.sem_objs[inc[0]], inc[1])
        block.tensor(lambda e: run(e, "pe"))
        block.scalar(lambda e: run(e, "act"))
        block.vector(lambda e: run(e, "dve"))
        block.gpsimd(lambda e: run(e, "pool"))
        block.sync(lambda e: run(e, "sp"))


VEC = {}
_off = 0


def _vreg(name, n):
    global _off
    VEC[name] = (_off, n)
    _off += n


_vreg("ada_b", 2 * 48)
_vreg("norm_mix", 2 * 8)
_vreg("norm_ffn", 2 * 8)
_vreg("norm_final", 8)
_vreg("conv_b", 2 * 44)
_vreg("conv_w", 2 * 44 * 9)
_vreg("mu", 8 * 12)
_vreg("w0", 2 * 8)
_vreg("a0", 2 * 8)
_vreg("k_k", 8)
_vreg("k_a", 8)
_vreg("r_k", 8)
_vreg("lnx_w", 8)
_vreg("lnx_b", 8)
NV = _off

CSTF = {"ident": (0, 128), "maskc": (128, 256)}
NCF = 384
CSTB = {}
_coff = 0


def _creg(name, n):
    global _coff
    CSTB[name] = (_coff, n)
    _coff += n


_creg("ident", 128)
_creg("blk1", 128)
_creg("blk64", 128)
_creg("ones1024", 128)
_creg("id2", 256)
_creg("mt0", 1024)
_creg("mt1", 1024)
_creg("mb0", 256)
_creg("mb1", 256)
NCB = _coff


def make_consts():
    c = np.zeros((128, NCF + NCB), np.float32)
    s = np.arange(128)[:, None]
    t = np.arange(128)[None, :]
    same = (s // 64) == (t // 64)
    c[:, 0:128] = np.eye(128)
    c[:, 128:384] = (np.arange(256) % 64 != 0)[None, :]

    def putb(name, arr):
        o, n = CSTB[name]
        c[:, NCF + o:NCF + o + n] = arr
    putb("ident", np.eye(128))
    putb("blk1", same)
    putb("blk64", same / 64.0)
    putb("ones1024", np.full((128, 128), 1.0 / 1024.0))
    putb("id2", np.concatenate([np.eye(128), np.eye(128)], 1))
    st0 = same & (s < t); in0 = same & (s <= t)
    st1 = same & (s > t); in1 = same & (s >= t)
    m0 = np.concatenate([st0, in0, st0, in0], 1)
    m1 = np.concatenate([st1, in1, st1, in1], 1)
    putb("mt0", np.concatenate([m0, m0], 1))
    putb("mt1", np.concatenate([m1, m1], 1))
    b0 = same & (t < s)
    b1 = same & (t > s)
    putb("mb0", np.concatenate([b0, b0], 1))
    putb("mb1", np.concatenate([b1, b1], 1))
    return c


class K:
    stop = None


class StopBuild(Exception):
    pass


def chk(name):
    if K.stop is not None and K.stop == name:
        raise StopBuild(name)


def build_program(debug=False):
    nc = bass.Bass("TRN2", target_bir_lowering=False)

    def din(name, shape):
        return nc.dram_tensor(name, list(shape), F32, kind="ExternalInput").ap()

    def dout(name, shape):
        return nc.dram_tensor(name, list(shape), F32, kind="ExternalOutput").ap()

    d_xp = din("xp", [2 * LP, D])
    d_xs = din("xs", [LS, D])
    d_st = din("st", [2, NHP, 128, 64])
    d_cond = din("cond", [128, 16])
    d_vec = din("vec", [128, NV])
    d_cst = din("cst", [128, NCF + NCB])
    d_ada = din("ada_w", [2, D, 6 * D])
    d_upp = din("ffn_upp", [2, NFC, 128, 8 * 2 * 128])
    d_dnp = din("ffn_dnp", [2, 8, 128, NFC * 128])
    d_wkvp = din("wkvp", [NHP, 128, 8 * 3 * 128])
    d_wop = din("wop", [8, 128, 8 * 128])
    d_w1 = din("w1", [2, D, 64]); d_w2 = din("w2", [2, 64, D])
    d_a1 = din("a1", [2, D, 64]); d_a2 = din("a2", [2, 64, D])
    d_g1 = din("g1", [D, 128]); d_g2 = din("g2", [128, D])
    d_sgin = din("sg_in", [D, 2 * E2]); d_sgop = din("sg_outp", [8, 128, 16 * 128])
    d_lnw = din("sg_lnw", [128, E2]); d_lnb = din("sg_lnb", [128, E2])
    d_bs = din("sg_bs", [128, 8]); d_wsT = din("sg_wsT", [128, 8 * 128])
    o_yp = dout("yp", [2 * LP, D])
    o_ys = dout("ys", [OUTS, D])
    o_ns = dout("ns", [2, 2, NHP, 128, 64])
    s_x = nc.dram_tensor("s_x", [128, NCH, TT], F32, kind="Internal").ap()
    s_y0 = nc.dram_tensor("s_y0", [2, NHP, 128, WIN], F32, kind="Internal").ap()

    ctx = ExitStack()
    AW = 52800
    arena = ctx.enter_context(nc.sbuf_tensor("arena", [128, AW], F32))
    PS = [ctx.enter_context(nc.psum_tensor("psb%d" % i, [128, 512], F32)) for i in range(8)]
    P = Prog(nc)

    class Arena:
        def __init__(self):
            self.top = 0
            self.hi = 0

        def f32(self, n):
            o = self.top
            self.top += n
            self.hi = max(self.hi, self.top)
            assert self.top <= AW, "SBUF arena overflow %d" % self.top
            return arena[:, o:o + n]

        def bf16(self, n):
            n2 = (n + 1) // 2
            return self.f32(n2).bitcast(BF16)[:, 0:n]

        def mark(self):
            return self.top

        def release(self, m):
            self.top = m
            P.barrier()
    A = Arena()

    def v3(ap, b):
        return ap.rearrange("p (a b) -> p a b", b=b)

    def v4(ap, b, c):
        return ap.rearrange("p (a b c) -> p a b c", b=b, c=c)

    def mm(out, lhsT, rhs, start, stop, reads, writes):
        rows = (int(lhsT.start_partition()), int(lhsT.partition_size()))
        P.op("pe", lambda e: e.matmul(out, lhsT, rhs, start=start, stop=stop), reads, writes, rows=rows)

    def tr(out, in_, ident, reads, writes):
        rows = (int(in_.start_partition()), int(in_.partition_size()))
        P.op("pe", lambda e: e.transpose(out, in_, ident), reads, writes, rows=rows)

    def act(out, in_, func, reads, writes, bias=None, scale=None):
        kw = {}
        if bias is not None:
            kw["bias"] = bias
        if scale is not None:
            kw["scale"] = scale
        P.op("act", lambda e: e.activation(out=out, in_=in_, func=func, **kw), reads, writes)

    def tt(eng, out, in0, in1, op, reads, writes):
        P.op(eng, lambda e: e.tensor_tensor(out, in0, in1, op), reads, writes)

    def ts(eng, out, in0, s1, op0, reads, writes, s2=None, op1=None):
        if op1 is None:
            P.op(eng, lambda e: e.tensor_scalar(out, in0, s1, None, op0), reads, writes)
        else:
            P.op(eng, lambda e: e.tensor_scalar(out, in0, s1, s2, op0, op1), reads, writes)

    def stt(out, in0, scalar, in1, op0, op1, reads, writes):
        P.op("dve", lambda e: e.scalar_tensor_tensor(out, in0, scalar, in1, op0, op1), reads, writes)

    def cp(eng, out, in_, reads, writes):
        if eng == "act":
            P.op("act", lambda e: e.copy(out, in_), reads, writes)
        else:
            P.op(eng, lambda e: e.tensor_copy(out, in_), reads, writes)

    def memset(eng, ap, val, writes):
        P.op(eng, lambda e: e.memset(ap, val), (), writes)

    cst = A.f32(NCF); B_cst = Buf("cst")
    vec = A.f32(NV); B_vec = Buf("vec")
    cstb = A.bf16(NCB); B_cstb = Buf("cstb")
    Hs0 = A.f32(2 * NHP * 64); B_Hs0 = Buf("Hs0")
    cond = A.f32(16); B_cond = Buf("cond")
    scond = A.bf16(16); B_scond = Buf("scond")
    mod = A.f32(2 * 96); B_mod = [Buf("mod0"), Buf("mod1")]
    gm = A.f32(2 * 2 * 16); B_gm = Buf("gm")

    def V(name, i=0, n=1):
        o, _ = VEC[name]
        return vec[:, o + i:o + i + n]

    def C(name, a=0, b=None):
        o, n = CSTF[name]
        return cst[:, o + a:o + (n if b is None else b)]

    def CB(name, a=0, b=None):
        o, n = CSTB[name]
        return cstb[:, o + a:o + (n if b is None else b)]

    def MOD(layer, m, ch, cd):
        o = layer * 96 + (m * 8 + ch) * 2 + cd
        return mod[:, o:o + 1]

    def GM(layer, which, ch, cd):
        o = layer * 32 + which * 16 + ch * 2 + cd
        return gm[:, o:o + 1]

    P.dma("sp", cst, d_cst[:, 0:NCF], writes=[B_cst], bulk="c")
    P.dma("sp", vec, d_vec, writes=[B_vec], bulk="c")
    P.dma("sp", cond, d_cond, writes=[B_cond], bulk="c")
    P.dma("sp", Hs0.rearrange("p (a i) -> p a i", i=64), d_st.rearrange("d h p i -> p (d h) i"), writes=[B_Hs0], bulk="c")
    P.dma("pool", cstb, d_cst[:, NCF:NCF + NCB], writes=[B_cstb], bulk="c")
    P.barrier()
    act(scond, cond, AF.Silu, [B_cond], [B_scond])

    PSB = [Buf("psb%d" % i, excl=True) for i in range(8)]

    def compute_mod(layer):
        m0 = A.mark()
        wb = [A.bf16(8 * 768) for _ in range(2)]
        Bw = [Buf("adaw0"), Buf("adaw1")]
        psm = PS[7][:, 0:96]
        for pc in range(8):
            sl = pc % 2
            src = d_ada[layer].rearrange("(kc p) f -> p kc f", p=128)[:, :, pc * 768:(pc + 1) * 768]
            P.dma("pool", v3(wb[sl], 768), src, writes=[Bw[sl]], sembuf=Bw[sl])
            wv_ = v3(wb[sl], 768)
            for fch in range(6):
                col = (pc * 6 + fch) * 2
                for kc in range(8):
                    mm(psm[:, col:col + 2], wv_[:, kc, fch * 128:(fch + 1) * 128], scond[:, kc * 2:kc * 2 + 2],
                       kc == 0, kc == 7, [Bw[sl], B_scond], [PSB[7]])
        modv = v3(mod[:, layer * 96:(layer + 1) * 96], 2)
        psv = v3(psm, 2)
        ab = V("ada_b", layer * 48, 48)
        for cd in range(2):
            tt("dve", modv[:, :, cd], psv[:, :, cd], ab, ALU.add, [PSB[7], B_vec], [B_mod[layer]])
        for which, (nm, msc) in enumerate((("norm_mix", 1), ("norm_ffn", 4))):
            for cd in range(2):
                o = layer * 32 + which * 16
                gv = v3(gm[:, o:o + 16], 2)[:, :, cd]
                scv = v3(mod[:, layer * 96 + msc * 16: layer * 96 + msc * 16 + 16], 2)[:, :, cd]
                ts("dve", gv, scv, 1.0, ALU.add, [B_mod[layer]], [B_gm])
                tt("dve", gv, gv, V(nm, layer * 8, 8), ALU.mult, [B_gm, B_vec], [B_gm])
        A.release(m0)

    def rmsnorm_fm(x_ap, n, Bx, scale_fn, bias_fn, out_fn, Bout, tmpbufs, psbank=6, extra_reads=()):
        sq, Bsq, rstd, Brs, tmp, Btmp = tmpbufs
        P.op("act", lambda e: e.activation(out=v3(sq[:, 0:8 * n], n), in_=x_ap, func=AF.Square), [Bx], [Bsq])
        ps = PS[psbank][:, 0:n]
        for c in range(8):
            mm(ps, CB("ones1024"), sq[:, c * n:(c + 1) * n], c == 0, c == 7, [Bsq, B_cstb], [PSB[psbank]])
        act(rstd[:, 0:n], ps, AF.Ln, [PSB[psbank]], [Brs], bias=RMS_EPS)
        act(rstd[:, 0:n], rstd[:, 0:n], AF.Exp, [Brs], [Brs], scale=-0.5)
        for c in range(8):
            eng = "dve" if c % 2 == 0 else "pool"
            tsl = tmp[c % 2][:, 0:n]
            tt(eng, tsl, x_ap[:, c, :], rstd[:, 0:n], ALU.mult, [Bx, Brs], [Btmp[c % 2]])
            b = bias_fn(c)
            if b is None:
                act(out_fn(c), tsl, AF.Copy, [Btmp[c % 2], B_gm, B_vec] + list(extra_reads), [Bout], scale=scale_fn(c))
            else:
                act(out_fn(c), tsl, AF.Identity, [Btmp[c % 2], B_gm, B_mod[0], B_mod[1], B_vec] + list(extra_reads),
                    [Bout], bias=b, scale=scale_fn(c))

    def norm_tmps(n):
        sq = A.bf16(8 * n)
        rstd = A.f32(n)
        tmp = [A.f32(n), A.f32(n)]
        return (sq, Buf("nsq"), rstd, Buf("nrstd"), tmp, [Buf("ntmp0"), Buf("ntmp1")])

    m_rwkv = A.mark()
    hpad = A.bf16(8 * HL); B_h = Buf("hpad")
    hv = v3(hpad, HL)
    P.op("pool", lambda e: e.memset(hpad, 0.0), (), [B_h])
    B_sx = Buf("s_x")
    B_sy0 = Buf("s_y0")

    def phase0():
        m0 = A.mark()
        xt = [A.f32(D) for _ in range(2)]
        Bxt = [Buf("xt0"), Buf("xt1")]
        xg = A.f32(8 * 512); Bxg = Buf("xg")
        xgv = v3(xg, 512)
        nt = norm_tmps(512)
        groups = [(d_xp, 0, [(HB_P[0], 256), (HB_P[1], 256)], 0, 0)]
        for g in range(4):
            groups.append((d_xs, g * 512, [(HB_S + g * 512, 512)], 1, (XB_S + g * 512) if g * 512 < WIN else None))
        ti = 0
        for (src, row0, pieces, cd, xb) in groups:
            for t4 in range(4):
                sl = ti % 2
                ti += 1
                P.dma("sp", xt[sl], src[row0 + t4 * 128: row0 + (t4 + 1) * 128, :], writes=[Bxt[sl]], sembuf=Bxt[sl])
                pb = 4 + (t4 % 2)
                for c in range(8):
                    bk = pb if c < 4 else pb + 2
                    tr(PS[bk][:, (c % 4) * 128:(c % 4 + 1) * 128], xt[sl][:, c * 128:(c + 1) * 128], C("ident"),
                       [Bxt[sl], B_cst], [PSB[bk]])
                for half in range(2):
                    bank = pb + 2 * half
                    eng = "act" if half == 0 else "dve"
                    cp(eng, xgv[:, half * 4:(half + 1) * 4, t4 * 128:(t4 + 1) * 128], v3(PS[bank][:, :], 128), [PSB[bank]], [Bxg])
            if xb is not None:
                nvalid = min(512, TT - xb)
                P.dma("sp", s_x[:, :, xb:xb + nvalid], xgv[:, :, 0:nvalid], reads=[Bxg], writes=[B_sx], bulk="sx")
            col = 0
            for (hb, ntok) in pieces:
                rmsnorm_fm(xgv[:, :, col:col + ntok], ntok, Bxg,
                           lambda c, cd=cd: GM(0, 0, c, cd), lambda c, cd=cd: MOD(0, 0, c, cd),
                           lambda c, hb=hb, ntok=ntok: hv[:, c, hb:hb + ntok], B_h, nt)
                col += ntok
        A.release(m0)

    def rwkv_layer():
        m0 = A.mark()
        B_w = Buf("rwkv_w")
        w1 = [A.bf16(8 * 64) for _ in range(2)]; a1 = [A.bf16(8 * 64) for _ in range(2)]
        w2 = [A.bf16(D) for _ in range(2)]; a2 = [A.bf16(D) for _ in range(2)]
        g1 = A.bf16(8 * 128); g2 = A.bf16(D)
        for d in range(2):
            P.dma("pool", v3(w1[d], 64), d_w1[d].rearrange("(kc p) f -> p kc f", p=128), writes=[B_w], bulk="rw")
            P.dma("pool", v3(a1[d], 64), d_a1[d].rearrange("(kc p) f -> p kc f", p=128), writes=[B_w], bulk="rw")
            P.dma("pool", w2[d][0:64, :], d_w2[d], writes=[B_w], bulk="rw")
            P.dma("pool", a2[d][0:64, :], d_a2[d], writes=[B_w], bulk="rw")
        P.dma("pool", v3(g1, 128), d_g1.rearrange("(kc p) f -> p kc f", p=128), writes=[B_w], bulk="rw")
        P.dma("pool", g2, d_g2, writes=[B_w], bulk="rw")
        wob = [A.bf16(8 * 128) for _ in range(2)]; B_wob = [Buf("wob0"), Buf("wob1")]
        mt2 = [CB("mt0"), CB("mt1")]; mb2 = [CB("mb0"), CB("mb1")]; id2 = CB("id2")
        Hf = A.f32(2 * NHP * 64); B_Hf = [[Buf("Hf%d_%d" % (d, h)) for h in range(NHP)] for d in range(2)]
        Hb = A.bf16(2 * 2 * NHP * 64)
        B_Hb = [[[Buf("Hb%d_%d_%d" % (i, d, h)) for h in range(NHP)] for d in range(2)] for i in range(2)]
        hb_par = [[0] * NHP for _ in range(2)]

        def HF(d, hp):
            o = (d * NHP + hp) * 64
            return Hf[:, o:o + 64]

        def HS0(d, hp):
            o = (d * NHP + hp) * 64
            return Hs0[:, o:o + 64]

        def HB(i, d, hp):
            o = ((i * 2 + d) * NHP + hp) * 64
            return Hb[:, o:o + 64]
        xs3 = {m: A.bf16(8 * NB) for m in ("r", "k", "v")}
        B_xs = {m: Buf("xs_" + m) for m in ("r", "k", "v", "t")}
        xst = A.bf16(8 * NB)
        dd1 = A.bf16(8 * NB); dd2 = A.bf16(8 * NB); B_dd = Buf("dd")
        tmpx = [A.f32(NB), A.f32(NB)]; B_tmpx = [Buf("tmpx0"), Buf("tmpx1")]
        twv = [[A.bf16(NB) for _ in range(2)] for _ in range(2)]; xa1v = [[A.bf16(NB) for _ in range(2)] for _ in range(2)]
        sggv = [A.bf16(NB) for _ in range(2)]
        B_twv = [[Buf("tw%d_%d" % (p, d)) for d in range(2)] for p in range(2)]
        B_xa1v = [[Buf("xa%d_%d" % (p, d)) for d in range(2)] for p in range(2)]
        B_sggv = [Buf("sgg0"), Buf("sgg1")]
        zbuf = A.bf16(8 * NB); B_z = [Buf("z%d" % h) for h in range(NHP)]
        xin = [A.f32(NB) for _ in range(2)]; B_xin = [Buf("xin0"), Buf("xin1")]
        xmid = [A.f32(NB) for _ in range(2)]; B_xmid = [Buf("xmid0"), Buf("xmid1")]

        class S_:
            pass

        def make_stream(si):
            S = S_()
            S.si = si
            S.b = [4 * si + i for i in range(4)]
            S.wkv = A.bf16(8 * 3 * 128); S.B_wkv = Buf("wkv%d" % si)
            S.wkvv = v4(S.wkv, 3, 128)
            S.ebuf = [A.f32(4 * 65) for _ in range(2)]; S.B_eb = [Buf("eb%d_0" % si), Buf("eb%d_1" % si)]
            for d in range(2):
                memset("pool", S.ebuf[d], 1.0, [S.B_eb[d]])
            names = ["r_f", "k_f", "kkraw", "v_f", "lns", "kk", "sgw", "asig", "lw", "lw2", "eneg", "t1", "kd",
                     "y0s", "b0s"]
            S.tf = {n: A.f32(NB) for n in names}
            S.Bt = {n: Buf("%s_%d" % (n, si)) for n in names}
            for al, src in (("ba", "t1"), ("y_f", "sgw"), ("yc", "asig"), ("yn", "lw"), ("tot", "lw2"), ("bon", "eneg")):
                S.tf[al] = S.tf[src]; S.Bt[al] = S.Bt[src]
            nbf = ["sq", "vT", "ktT", "btT", "rkd", "ybf", "sqb"]
            S.tb = {n: A.bf16(NB) for n in nbf}
            for n in nbf:
                S.Bt[n] = Buf("%s_%d" % (n, si))
            S.arT = A.bf16(2 * 2 * 128); S.Bt["arT"] = Buf("arT%d" % si)
            S.arv = v4(S.arT, 2, 128)
            S.tmb = A.bf16(8 * 128); S.Bt["tmb"] = Buf("tmb%d" % si)
            S.tmv = v4(S.tmb, 4, 128)
            S.Bt["Vtm"] = Buf("Vtm%d" % si)
            S.Vtv = None
            S.GB = A.bf16(256); S.Bt["GB"] = Buf("GB%d" % si); S.GBv = v3(S.GB, 128)
            S.Pb = [[A.bf16(384) for _ in range(2)] for _ in range(2)]
            S.BPb = [[Buf("Pb%d_%d_%d" % (si, e, i)) for i in range(2)] for e in range(2)]
            S.Qb = [[None, A.bf16(128)] for _ in range(2)]
            S.BQb = [[Buf("Qb%d_%d_%d" % (si, e, i)) for i in range(2)] for e in range(2)]
            S.BQ0 = [Buf("Q0_%d_%d" % (si, e)) for e in range(2)]
            S.XV = A.bf16(128); S.Bt["XV"] = Buf("XV%d" % si)
            S.WmT = [A.bf16(128) for _ in range(2)]; S.BWm = [Buf("WmT%d_0" % si), Buf("WmT%d_1" % si)]
            S.U0 = [A.f32(128) for _ in range(2)]; S.BU0 = [Buf("U0%d_0" % si), Buf("U0%d_1" % si)]
            S.Ub = [A.bf16(128) for _ in range(2)]; S.BUb = [Buf("Ub%d_0" % si), Buf("Ub%d_1" % si)]
            S.GAs = [A.bf16(1024) for _ in range(2)]; S.BGAs = [Buf("GAs%d_0" % si), Buf("GAs%d_1" % si)]
            S.htmp = A.f32(64); S.Bt["htmp"] = Buf("htmp%d" % si)
            return S
        ST = [make_stream(0), make_stream(1)]

        def load_wkv(S, hp):
            P.dma("pool", S.wkv, d_wkvp[hp], writes=[S.B_wkv], sembuf=S.B_wkv)
        load_wkv(ST[0], 0)
        load_wkv(ST[1], 1)
        P.barrier()

        def half(b, h):
            return PS[b][:, h * 256:(h + 1) * 256]

        def head_rows(e):
            return slice(64 * e, 64 * e + 64)

        mixidx = {"r": 0, "w": 1, "k": 2, "v": 3, "a": 4, "g": 5}
        wo_cnt = [0]
        proj_done = [0]
        STREAM_LAG = 0

        def hp_gen(S, hp, dirs, final, need_y, wb0, vp):
            tf, tb, Bt = S.tf, S.tb, S.Bt
            tw, xa1, sgg = twv[vp], xa1v[vp], sggv[vp]
            B_tw, B_xa1, B_sgg = B_twv[vp], B_xa1v[vp], B_sggv[vp]
            arv, tmv, Vtv = S.arv, S.tmv, S.Vtv
            b0, b1, b2, b3 = S.b
            PB = PSB
            hc0 = hp * 128
            first_dir = dirs[0]
            ps_k = half(b0, 0); ps_v = half(b2, 0); ps_r = half(b1, 0)
            for c in range(8):
                mm(ps_k, S.wkvv[:, c, 1, :], v3(xs3["k"], NB)[:, c, :], c == 0, c == 7, [S.B_wkv, B_xs["k"]], [PB[b0]])
            for c in range(8):
                mm(ps_v, S.wkvv[:, c, 2, :], v3(xs3["v"], NB)[:, c, :], c == 0, c == 7, [S.B_wkv, B_xs["v"]], [PB[b2]])
            if need_y:
                for c in range(8):
                    mm(ps_r, S.wkvv[:, c, 0, :], v3(xs3["r"], NB)[:, c, :], c == 0, c == 7, [S.B_wkv, B_xs["r"]], [PB[b1]])
            if final and need_y and len(dirs) == 1:
                P.dma("sp", tf["y0s"], s_y0[0, hp, :, wb0:wb0 + NB], reads=[B_sy0], writes=[Bt["y0s"]], sembuf=Bt["y0s"])
                P.dma("sp", tf["b0s"], s_y0[1, hp, :, wb0:wb0 + NB], reads=[B_sy0], writes=[Bt["b0s"]], sembuf=Bt["b0s"])
            load_wkv(S, (hp + 2) % NHP)
            proj_done[0] += 1
            yield
            if need_y:
                cp("act", tf["r_f"], ps_r, [PB[b1]], [Bt["r_f"]])
            act(tf["kkraw"], ps_k, AF.Copy, [PB[b0], B_vec], [Bt["kkraw"]], scale=V("k_k", hp))
            act(tb["sq"], ps_k, AF.Square, [PB[b0], B_vec], [Bt["sq"]], scale=V("k_k", hp))
            cp("dve", tf["v_f"], ps_v, [PB[b2]], [Bt["v_f"]])
            cp("dve", tf["k_f"], ps_k, [PB[b0]], [Bt["k_f"]])
            cp("act", tb["vT"], ps_v, [PB[b2]], [Bt["vT"]])
            yield
            ps_ss = half(b1, 1)
            mm(ps_ss, CB("blk1"), tb["sq"], True, True, [B_cstb, Bt["sq"]], [PB[b1]])
            yield
            act(tf["lns"], ps_ss, AF.Ln, [PB[b1]], [Bt["lns"]], bias=1e-30)
            act(tf["lns"], tf["lns"], AF.Exp, [Bt["lns"]], [Bt["lns"]], scale=-0.5)
            yield
            tt("dve", tf["kk"], tf["kkraw"], tf["lns"], ALU.mult, [Bt["kkraw"], Bt["lns"]], [Bt["kk"]])
            yield
            for d in dirs:
                ebuf = S.ebuf; B_eb = S.B_eb
                ps_w = half(b0, 0); ps_a = half(b0, 1)
                mm(ps_w, w2[d][0:64, hc0:hc0 + 128], tw[d][0:64, :], True, True, [B_w, B_tw[d]], [PB[b0]])
                mm(ps_a, a2[d][0:64, hc0:hc0 + 128], xa1[d][0:64, :], True, True, [B_w, B_xa1[d]], [PB[b0]])
                yield
                act(tf["sgw"], ps_w, AF.Sigmoid, [PB[b0], B_vec], [Bt["sgw"]], bias=V("w0", d * 8 + hp))
                act(tf["asig"], ps_a, AF.Sigmoid, [PB[b0], B_vec], [Bt["asig"]], bias=V("a0", d * 8 + hp))
                yield
                P.op("dve", lambda e: e.tensor_tensor_scan(tf["lw"], C("maskc"), tf["sgw"], 0.0, ALU.mult, ALU.add),
                     [B_cst, Bt["sgw"]], [Bt["lw"]])
                ts("dve", tf["t1"], tf["asig"], -1.0, ALU.add, [Bt["asig"], B_vec], [Bt["t1"]], s2=V("k_a", hp), op1=ALU.mult)
                stt(tf["kd"], tf["t1"], 1.0, tf["k_f"], ALU.add, ALU.mult, [Bt["t1"], Bt["k_f"]], [Bt["kd"]])
                yield
                if d == 0:
                    lwx = tf["lw"]; Blwx = Bt["lw"]
                else:
                    tt("pool", tf["t1"], tf["sgw"], tf["lw"], ALU.subtract, [Bt["sgw"], Bt["lw"]], [Bt["t1"]])
                    yield
                    totb = v3(tf["lw"], 64)[:, :, 63:64].broadcast_to([128, 4, 64])
                    tt("dve", v3(tf["lw2"], 64), v3(tf["t1"], 64), totb, ALU.add, [Bt["t1"], Bt["lw"]], [Bt["lw2"]])
                    yield
                    lwx = tf["lw2"]; Blwx = Bt["lw2"]
                ebv = v3(ebuf[d], 65)
                if d == 0:
                    epos = ebv[:, :, 1:65]; eprev = ebv[:, :, 0:64]; cwcol = 64
                else:
                    epos = ebv[:, :, 0:64]; eprev = ebv[:, :, 1:65]; cwcol = 0
                act(epos, v3(lwx, 64), AF.Exp, [Blwx], [B_eb[d]], scale=-CDEC)
                act(tf["eneg"], lwx, AF.Exp, [Blwx], [Bt["eneg"]], scale=CDEC)
                tt("dve", tf["ba"], tf["kk"], tf["asig"], ALU.mult, [Bt["kk"], Bt["asig"]], [Bt["ba"]])
                yield
                tt("pool", tb["ktT"], tf["kd"], tf["eneg"], ALU.mult, [Bt["kd"], Bt["eneg"]], [Bt["ktT"]])
                a_out = arv[:, :, 0, :].rearrange("p t (c j) -> p t c j", j=64)
                r_out = arv[:, :, 1, :].rearrange("p t (c j) -> p t c j", j=64)
                kk4 = tf["kk"].rearrange("p (t c j) -> p t c j", c=2, j=64)
                ep4 = eprev.rearrange("p (t c) j -> p t c j", c=2)
                eo4 = epos.rearrange("p (t c) j -> p t c j", c=2)
                stt(a_out, kk4, -1.0, ep4, ALU.mult, ALU.mult, [Bt["kk"], B_eb[d]], [Bt["arT"]])
                yield
                tt("pool", tb["btT"], tf["ba"], tf["eneg"], ALU.mult, [Bt["ba"], Bt["eneg"]], [Bt["btT"]])
                if need_y:
                    r4 = tf["r_f"].rearrange("p (t c j) -> p t c j", c=2, j=64)
                    tt("dve", r_out, r4, eo4, ALU.mult, [Bt["r_f"], B_eb[d]], [Bt["arT"]])
                    stt(tb["rkd"], tf["kd"], V("r_k", hp), tf["r_f"], ALU.mult, ALU.mult, [Bt["kd"], Bt["r_f"], B_vec], [Bt["rkd"]])
                yield
                pst = PS[b3].bitcast(BF16)
                for tl in range(2):
                    srcs = [tb["btT"][:, tl * 128:(tl + 1) * 128], tb["ktT"][:, tl * 128:(tl + 1) * 128], arv[:, tl, 0, :]]
                    rds = [[Bt["btT"]], [Bt["ktT"]], [Bt["arT"]]]
                    for ki in range(3):
                        tr(pst[:, (tl * 4 + ki) * 128:(tl * 4 + ki + 1) * 128], srcs[ki], CB("ident"), rds[ki] + [B_cstb], [PB[b3]])
                    if d == first_dir:
                        tr(pst[:, (tl * 4 + 3) * 128:(tl * 4 + 4) * 128], tb["vT"][:, tl * 128:(tl + 1) * 128], CB("ident"),
                           [Bt["vT"], B_cstb], [PB[b3]])
                yield
                pst4 = v4(pst, 4, 128)
                if d == first_dir:
                    cp("act", tmv[:, :, :, :], pst4[:, :, :, :], [PB[b3]], [Bt["tmb"], Bt["Vtm"]])
                else:
                    cp("act", tmv[:, :, 0:3, :], pst4[:, :, 0:3, :], [PB[b3]], [Bt["tmb"]])
                yield
                Vt = lambda tl, e: tmv[:, tl, 3, e * 64:(e + 1) * 64]
                tiles = [0, 1] if d == 0 else [1, 0]
                for tl in tiles:
                    gasv = v3(S.GAs[tl], 512)
                    for e in range(2):
                        R = head_rows(e)
                        bA = b1 if e == 0 else b3
                        psA = PS[bA]
                        bT = tb["btT"][R, tl * 128:(tl + 1) * 128]
                        kT = tb["ktT"][R, tl * 128:(tl + 1) * 128]
                        ar = arv[R, tl, :, :]
                        aT = arv[R, tl, 0, :]
                        psBe = PS[b0][:, 384:512]
                        mm(psA[:, 0:256], bT, ar, True, True, [Bt["btT"], Bt["arT"]], [PB[bA]])
                        mm(psA[:, 256:512], kT, ar, True, True, [Bt["ktT"], Bt["arT"]], [PB[bA]])
                        mm(psBe, aT, bT, True, True, [Bt["btT"], Bt["arT"]], [PB[b0]])
                        yield
                        tt("dve", gasv[:, e, :], psA[:, :], mt2[d][:, e * 512:(e + 1) * 512], ALU.mult, [PB[bA], B_cstb], [S.BGAs[tl]])
                        tt("dve", S.GBv[:, e, :], psBe, mb2[d][:, e * 128:(e + 1) * 128], ALU.mult, [PB[b0], B_cstb], [Bt["GB"]])
                        yield
                        tt("pool", S.Pb[e][1][:, 256:384], gasv[:, e, 0:128], id2[:, 0:128], ALU.add, [S.BGAs[tl], B_cstb], [S.BQ0[e]])
                    psx = PS[b0][:, 0:128]
                    for e in range(2):
                        mm(psx[:, e * 64:(e + 1) * 64], gasv[:, e, 256:384], Vt(tl, e), True, True, [S.BGAs[tl], Bt["tmb"]], [PB[b0]])
                    yield
                    cp("act", S.XV, psx, [PB[b0]], [Bt["XV"]])
                    for lev in range(1, 6):
                        pbi = lev % 2
                        for e in range(2):
                            bk = b2 + e
                            if lev == 1:
                                Pm = S.GBv[:, e, :]; PTm = gasv[:, e, 0:128]; rr = [Bt["GB"], S.BGAs[tl]]
                                mm(PS[bk][:, 0:128], PTm, Pm, True, True, rr, [PB[bk]])
                                mm(PS[bk][:, 128:256], Pm, PTm, True, True, rr, [PB[bk]])
                            else:
                                src = S.Pb[e][1 - pbi]
                                rr = [S.BPb[e][1 - pbi]] + ([S.BQ0[e]] if lev == 2 else [])
                                mm(PS[bk][:, 0:128], src[:, 128:256], src[:, 0:128], True, True, rr, [PB[bk]])
                                mm(PS[bk][:, 128:384], src[:, 0:128], src[:, 128:384], True, False, rr, [PB[bk]])
                                mm(PS[bk][:, 256:384], CB("ident"), src[:, 256:384], False, True, rr + [B_cstb], [PB[bk]])
                        yield
                        for e in range(2):
                            bk = b2 + e
                            n_ = 256 if lev == 1 else 384
                            cp("act" if e == 0 else "dve", S.Pb[e][pbi][:, 0:n_], PS[bk][:, 0:n_], [PB[bk]], [S.BPb[e][pbi]])
                        yield
                    for e in range(2):
                        bk = b2 + e
                        src = S.Pb[e][1]
                        mm(PS[bk][:, 256:384], src[:, 0:128], src[:, 256:384], True, False, [S.BPb[e][1]], [PB[bk]])
                        mm(PS[bk][:, 256:384], CB("ident"), src[:, 256:384], False, True, [S.BPb[e][1], B_cstb], [PB[bk]])
                    yield
                    for e in range(2):
                        bk = b2 + e
                        cp("act" if e == 0 else "dve", S.Qb[e][1], PS[bk][:, 256:384], [PB[bk]], [S.BQb[e][1]])
                    yield
                    TT_ = [S.Qb[0][1], S.Qb[1][1]]
                    BTT = [S.BQb[0][1], S.BQb[1][1]]
                    psw = PS[b0][:, 128:256]
                    for e in range(2):
                        mm(psw[head_rows(e), :], tmv[:, tl, 2, e * 64:(e + 1) * 64], TT_[e], True, True, [Bt["tmb"], BTT[e]], [PB[b0]])
                    psu0 = PS[b0][:, 256:384]
                    for e in range(2):
                        mm(psu0[:, e * 64:(e + 1) * 64], TT_[e], S.XV[:, e * 64:(e + 1) * 64], True, True, [BTT[e], Bt["XV"]], [PB[b0]])
                    yield
                    cp("act", S.WmT[tl], psw, [PB[b0]], [S.BWm[tl]])
                    cp("act", S.U0[tl], psu0, [PB[b0]], [S.BU0[tl]])
                    yield
                psY = PS[b1][:, 256:512]
                for tl in tiles:
                    gasv = v3(S.GAs[tl], 512)
                    chunks = [0, 1] if d == 0 else [1, 0]
                    for cq in chunks:
                        Sc = slice(64 * cq, 64 * cq + 64)
                        chl = tl * 2 + cq
                        cur = hb_par[d][hp]
                        nxt = 1 - cur
                        psU = PS[b0][:, 0:128]
                        for e in range(2):
                            R = head_rows(e)
                            mm(psU[Sc, e * 64:(e + 1) * 64], S.WmT[tl][R, cq * 64:(cq + 1) * 64], HB(cur, d, hp)[R, :], True, True,
                               [S.BWm[tl], B_Hb[cur][d][hp]], [PB[b0]])
                        yield
                        tt("dve", S.Ub[tl][Sc, :], psU[Sc, :], S.U0[tl][Sc, :], ALU.add, [PB[b0], S.BU0[tl]], [S.BUb[tl]])
                        yield
                        psH = PS[b3][:, 128:192]
                        for e in range(2):
                            R = head_rows(e)
                            mm(psH[R, :], tmv[Sc, tl, 0, e * 64:(e + 1) * 64], S.Ub[tl][Sc, e * 64:(e + 1) * 64], True, False,
                               [Bt["tmb"], S.BUb[tl]], [PB[b3]])
                            mm(psH[R, :], tmv[Sc, tl, 1, e * 64:(e + 1) * 64], tmv[Sc, tl, 3, e * 64:(e + 1) * 64], False, True,
                               [Bt["tmb"]], [PB[b3]])
                        if need_y:
                            yc0 = tl * 128 + cq * 64
                            for e in range(2):
                                R = head_rows(e)
                                mm(psY[R, yc0:yc0 + 64], HB(cur, d, hp)[R, :], arv[R, tl, 1, cq * 64:(cq + 1) * 64], True, False,
                                   [B_Hb[cur][d][hp], Bt["arT"]], [PB[b1]])
                                mm(psY[R, yc0:yc0 + 64], S.Ub[tl][Sc, e * 64:(e + 1) * 64],
                                   gasv[Sc, e, 128 + cq * 64:128 + cq * 64 + 64], False, False, [S.BUb[tl], S.BGAs[tl]], [PB[b1]])
                                mm(psY[R, yc0:yc0 + 64], tmv[Sc, tl, 3, e * 64:(e + 1) * 64],
                                   gasv[Sc, e, 384 + cq * 64:384 + cq * 64 + 64], False, True, [Bt["tmb"], S.BGAs[tl]], [PB[b1]])
                        yield
                        cw = v3(ebuf[d], 65)[:, chl, cwcol:cwcol + 1]
                        act(S.htmp, HF(d, hp), AF.Copy, [B_Hf[d][hp], B_eb[d]], [Bt["htmp"]], scale=cw)
                        yield
                        stt(HB(nxt, d, hp), psH, cw, S.htmp, ALU.mult, ALU.add, [PB[b3], B_eb[d], Bt["htmp"]], [B_Hb[nxt][d][hp]])
                        stt(HF(d, hp), psH, cw, S.htmp, ALU.mult, ALU.add, [PB[b3], B_eb[d], Bt["htmp"]], [B_Hf[d][hp]])
                        hb_par[d][hp] = nxt
                        yield
                if need_y:
                    ps_s = half(b0, 0)
                    mm(ps_s, CB("blk1"), tb["rkd"], True, True, [B_cstb, Bt["rkd"]], [PB[b0]])
                    is_first = (d == dirs[0]) and not (final and len(dirs) == 1)
                    if is_first:
                        cp("act", tf["y0s"], psY, [PB[b1]], [Bt["y0s"]])
                        yield
                        tt("dve", tf["b0s"], ps_s, tf["v_f"], ALU.mult, [PB[b0], Bt["v_f"]], [Bt["b0s"]])
                        if not final:
                            P.dma("sp", s_y0[0, hp, :, wb0:wb0 + NB], tf["y0s"], reads=[Bt["y0s"]], writes=[B_sy0], sembuf=Bt["y0s"])
                            P.dma("sp", s_y0[1, hp, :, wb0:wb0 + NB], tf["b0s"], reads=[Bt["b0s"]], writes=[B_sy0], sembuf=Bt["b0s"])
                        yield
                    else:
                        tt("dve", tf["y_f"], psY, tf["y0s"], ALU.add, [PB[b1], Bt["y0s"]], [Bt["y_f"]])
                        yield
                        cp("act", tb["ybf"], tf["y_f"], [Bt["y_f"]], [Bt["ybf"]])
                        tt("dve", tf["bon"], ps_s, tf["v_f"], ALU.mult, [PB[b0], Bt["v_f"]], [Bt["bon"]])
                        yield
                        ps_m = half(b1, 0)
                        mm(ps_m, CB("blk64"), tb["ybf"], True, True, [B_cstb, Bt["ybf"]], [PB[b1]])
                        yield
                        tt("dve", tf["yc"], tf["y_f"], ps_m, ALU.subtract, [Bt["y_f"], PB[b1]], [Bt["yc"]])
                        yield
                        act(tb["sqb"], tf["yc"], AF.Square, [Bt["yc"]], [Bt["sqb"]])
                        yield
                        ps_v2 = half(b1, 1)
                        mm(ps_v2, CB("blk64"), tb["sqb"], True, True, [B_cstb, Bt["sqb"]], [PB[b1]])
                        yield
                        act(tf["yn"], ps_v2, AF.Ln, [PB[b1]], [Bt["yn"]], bias=GN_EPS)
                        act(tf["yn"], tf["yn"], AF.Exp, [Bt["yn"]], [Bt["yn"]], scale=-0.5)
                        yield
                        tt("pool", tf["yn"], tf["yc"], tf["yn"], ALU.mult, [Bt["yc"], Bt["yn"]], [Bt["yn"]])
                        yield
                        ts("dve", tf["tot"], tf["yn"], V("lnx_w", hp), ALU.mult, [Bt["yn"], B_vec], [Bt["tot"]],
                           s2=V("lnx_b", hp), op1=ALU.add)
                        ps_g = half(b0, 1)
                        mm(ps_g, g2[:, hc0:hc0 + 128], sgg, True, True, [B_w, B_sgg], [PB[b0]])
                        yield
                        tt("pool", tf["tot"], tf["tot"], tf["b0s"], ALU.add, [Bt["tot"], Bt["b0s"]], [Bt["tot"]])
                        yield
                        tt("pool", tf["tot"], tf["tot"], tf["bon"], ALU.add, [Bt["tot"], Bt["bon"]], [Bt["tot"]])
                        yield
                        tt("dve", v3(zbuf, NB)[:, hp, :], tf["tot"], ps_g, ALU.mult, [Bt["tot"], PB[b0]], [B_z[hp]])
                        yield

        def vis_params(kind, si, blk):
            if kind == "p":
                return HB_P[si] + blk * NB, XB_P[si] + blk * NB, 0
            return HB_S + blk * NB, XB_S + blk * NB, 1

        def prologue_gen(vis, vp):
            kind, si, blk, dirs, final, need_y = vis
            hb0, xb0, cd = vis_params(kind, si, blk)
            tw, xa1, sgg = twv[vp], xa1v[vp], sggv[vp]
            B_tw, B_xa1, B_sgg = B_twv[vp], B_xa1v[vp], B_sggv[vp]
            hc = hv[:, :, hb0:hb0 + NB]; hp_ = hv[:, :, hb0 - 1:hb0 - 1 + NB]; hn = hv[:, :, hb0 + 1:hb0 + 1 + NB]
            tt("dve", v3(dd1, NB), hp_, hc, ALU.subtract, [B_h], [B_dd])
            tt("pool", v3(dd2, NB), hn, hc, ALU.subtract, [B_h], [B_dd])
            yield

            def make_xs(m, dst, Bdst):
                mi = mixidx[m]
                dv = v3(dst, NB)
                for c in range(8):
                    mu0 = V("mu", c * 12 + mi); mu1 = V("mu", c * 12 + 6 + mi)
                    tsl = tmpx[c % 2]
                    stt(tsl, v3(dd1, NB)[:, c, :], mu0, hc[:, c, :], ALU.mult, ALU.add, [B_dd, B_h, B_vec], [B_tmpx[c % 2]])
                    stt(dv[:, c, :], v3(dd2, NB)[:, c, :], mu1, tsl, ALU.mult, ALU.add, [B_dd, B_tmpx[c % 2], B_vec], [Bdst])
                    if c % 2 == 1:
                        yield
            need = ["k", "v"] + (["r"] if need_y else [])
            for m in need:
                yield from make_xs(m, xs3[m], B_xs[m])
            yield from make_xs("w", xst, B_xs["t"])
            pro_ps = [(PS[2][:, 384:512], PSB[2]), (PS[6][:, 384:512], PSB[6])]
            for d in dirs:
                for hf in range(2):
                    ps_, Bp_ = pro_ps[hf]
                    for c in range(8):
                        mm(ps_[0:64, :], v3(w1[d], 64)[:, c, :], v3(xst, NB)[:, c, hf * 128:(hf + 1) * 128], c == 0, c == 7,
                           [B_w, B_xs["t"]], [Bp_])
                    act(tw[d][0:64, hf * 128:(hf + 1) * 128], ps_[0:64, :], AF.Tanh, [Bp_], [B_tw[d]])
                    yield
            yield from make_xs("a", xst, B_xs["t"])
            for d in dirs:
                for hf in range(2):
                    ps_, Bp_ = pro_ps[hf]
                    for c in range(8):
                        mm(ps_[0:64, :], v3(a1[d], 64)[:, c, :], v3(xst, NB)[:, c, hf * 128:(hf + 1) * 128], c == 0, c == 7,
                           [B_w, B_xs["t"]], [Bp_])
                    cp("act", xa1[d][0:64, hf * 128:(hf + 1) * 128], ps_[0:64, :], [Bp_], [B_xa1[d]])
                    yield
            if final:
                yield from make_xs("g", xst, B_xs["t"])
                for hf in range(2):
                    ps_, Bp_ = pro_ps[hf]
                    for c in range(8):
                        mm(ps_, v3(g1, 128)[:, c, :], v3(xst, NB)[:, c, hf * 128:(hf + 1) * 128], c == 0, c == 7,
                           [B_w, B_xs["t"]], [Bp_])
                    act(sgg[:, hf * 128:(hf + 1) * 128], ps_, AF.Sigmoid, [Bp_], [B_sgg])
                    yield

        def run_gens(gens):
            alive = [True] * len(gens)
            while any(alive):
                for gi in range(len(gens)):
                    if alive[gi]:
                        try:
                            next(gens[gi])
                        except StopIteration:
                            alive[gi] = False

        def visit_main(vis, vp, nxt_vis):
            kind, si, blk, dirs, final, need_y = vis
            hb0, xb0, cd = vis_params(kind, si, blk)
            wb0 = blk * NB
            def stream_chain(S, hps, lag):
                for _ in range(lag):
                    yield
                for hp in hps:
                    yield from hp_gen(S, hp, dirs, final, need_y, wb0, vp)

            def late_prologue():
                while proj_done[0] < NHP:
                    yield
                if nxt_vis is not None:
                    yield from prologue_gen(nxt_vis, 1 - vp)
            proj_done[0] = 0
            run_gens([stream_chain(ST[0], [0, 2, 4, 6], 0), stream_chain(ST[1], [1, 3, 5, 7], STREAM_LAG), late_prologue()])
            if final:
                for dc in range(8):
                    sl = wo_cnt[0] % 2
                    wo_cnt[0] += 1
                    if wo_cnt[0] == 1:
                        P.dma("pool", wob[0], d_wop[0], writes=[B_wob[0]], sembuf=B_wob[0])
                    P.dma("pool", wob[1 - sl], d_wop[(dc + 1) % 8], writes=[B_wob[1 - sl]], sembuf=B_wob[1 - sl])
                    P.dma("sp", xin[sl], s_x[:, dc, xb0:xb0 + NB], reads=[B_sx], writes=[B_xin[sl]], sembuf=B_xin[sl])
                    pso = PS[dc % 2][:, 256:512]
                    for hp in range(NHP):
                        mm(pso, v3(wob[sl], 128)[:, hp, :], v3(zbuf, NB)[:, hp, :], hp == 0, hp == NHP - 1, [B_wob[sl], B_z[hp]], [PSB[dc % 2]])
                    stt(xmid[sl], pso, MOD(0, 2, dc, cd), xin[sl], ALU.mult, ALU.add,
                        [PSB[dc % 2], B_mod[0], B_xin[sl]], [B_xmid[sl]])
                    P.dma("sp", s_x[:, dc, xb0:xb0 + NB], xmid[sl], reads=[B_xmid[sl]], writes=[B_sx], sembuf=B_xmid[sl])

        def init_state(kind):
            for d in range(2):
                for hp in range(NHP):
                    if kind == "p":
                        memset("pool", HF(d, hp), 0.0, [B_Hf[d][hp]])
                    else:
                        cp("pool", HF(d, hp), HS0(d, hp), [B_Hs0], [B_Hf[d][hp]])
                    cur = hb_par[d][hp]
                    cp("pool", HB(cur, d, hp), HF(d, hp), [B_Hf[d][hp]], [B_Hb[cur][d][hp]])

        B_ns = Buf("ns")
        visits = []
        for si in range(2):
            visits.append(("p", si, 0, [0, 1], True, True))
        for blk in range(WIN // NB):
            visits.append(("s", 0, blk, [0], False, True))
        for blk in range(LS // NB - 1, WIN // NB - 1, -1):
            visits.append(("s", 0, blk, [1], False, False))
        for blk in range(WIN // NB - 1, -1, -1):
            visits.append(("s", 0, blk, [1], True, True))
        run_gens([prologue_gen(visits[0], 0)])
        for i, vis in enumerate(visits):
            vp = i % 2
            if vis[0] == "p":
                init_state("p")
            elif i == 2:
                init_state("s")
            visit_main(vis, vp, visits[i + 1] if i + 1 < len(visits) else None)
            if vis[0] == "p":
                for d in range(2):
                    for hp in range(NHP):
                        P.dma("sp", o_ns[vis[1], d, hp], HF(d, hp), reads=[B_Hf[d][hp]], writes=[B_ns], bulk="out")
        A.release(m0)


    def load_x(xall, B_x):
        P.dma("sp", v3(xall, TT), s_x, reads=[B_sx], writes=[B_x], bulk="lx")

    def ffn_layer(layer, n_s_up, n_s_dn):
        m0 = A.mark()
        NTK = 2 * LP + n_s_up
        h2 = A.bf16(8 * NTK); B_h2 = Buf("h2")
        h2v = v3(h2, NTK)
        actb = A.bf16(NFC * NTK); B_act = Buf("act")
        actv = v3(actb, NTK)
        m1 = A.mark()
        xg2 = [A.f32(8 * 512) for _ in range(2)]; Bxg2 = [Buf("xg_f0"), Buf("xg_f1")]
        nt = norm_tmps(512)
        pieces = []
        col = 0
        while col < NTK:
            n = min(512, NTK - col)
            pieces.append((col, n))
            col += n

        def ld(i):
            c_, n_ = pieces[i]
            P.dma("sp", v3(xg2[i % 2], 512)[:, :, 0:n_], s_x[:, :, c_:c_ + n_], reads=[B_sx], writes=[Bxg2[i % 2]], sembuf=Bxg2[i % 2])
        ld(0)
        for i, (col, n) in enumerate(pieces):
            if i + 1 < len(pieces):
                ld(i + 1)
            cd = 0 if col < 2 * LP else 1
            rmsnorm_fm(v3(xg2[i % 2], 512)[:, :, 0:n], n, Bxg2[i % 2], lambda c, cd=cd: GM(layer, 1, c, cd),
                       lambda c, cd=cd: MOD(layer, 3, c, cd), lambda c, col=col, n=n: h2v[:, c, col:col + n], B_h2, nt)
        A.release(m1)
        rows = n_s_up // 64
        UPW = (rows + 2) * 66
        upad_s = [A.bf16(UPW) for _ in range(2)]; upad_p = [A.bf16(2 * 258) for _ in range(2)]
        B_up_s = [Buf("ups0"), Buf("ups1")]; B_up_p = [Buf("upp0"), Buf("upp1")]
        for i in range(2):
            memset("pool", upad_s[i], 0.0, [B_up_s[i]])
            memset("pool", upad_p[i], 0.0, [B_up_p[i]])
        wup = [A.bf16(8 * 256) for _ in range(2)]; B_wup = [Buf("wup0"), Buf("wup1")]
        dg = [A.bf16(2 * 9 * 128) for _ in range(2)]; B_dg = [Buf("dg0"), Buf("dg1")]
        vv = [A.f32(512) for _ in range(2)]; B_vv = [Buf("vv0"), Buf("vv1")]
        sg = [A.f32(512) for _ in range(2)]; B_sg = [Buf("sg0"), Buf("sg1")]
        rows_c = n_s_dn // 64
        rows_u = min(rows, rows_c + 1)

        def mk_tiles(nrows):
            t_, r0_ = [], 0
            while r0_ < nrows:
                nr_ = min(8, nrows - r0_)
                t_.append((r0_, nr_))
                r0_ += nr_
            return t_
        s_tiles = mk_tiles(rows_u)
        c_tiles = mk_tiles(rows_c)
        it = 0
        P.dma("pool", wup[0], d_upp[layer, 0], writes=[B_wup[0]], sembuf=B_wup[0])
        for j in range(NFC):
            sl = j % 2
            wv_ = v4(wup[sl], 2, 128)
            if j + 1 < NFC:
                P.dma("pool", wup[1 - sl], d_upp[layer, j + 1], writes=[B_wup[1 - sl]], sembuf=B_wup[1 - sl])
            dgv = v4(dg[sl], 9, 128)
            for vg in range(2):
                fc = j + vg * NFC
                idb = C("ident").unsqueeze(1).broadcast_to([128, 9, 128])
                wcb = V("conv_w", (layer * 44 + fc) * 9, 9).unsqueeze(2).broadcast_to([128, 9, 128])
                tt("dve", dgv[:, vg, :, :], idb, wcb, ALU.mult, [B_cst, B_vec], [B_dg[sl]])
            for vg in range(2):
                ps = PS[vg]
                for c in range(8):
                    mm(ps[:, 0:512], wv_[:, c, vg, :], h2v[:, c, 0:512], c == 0, c == 7, [B_wup[sl], B_h2], [PSB[vg]])
                upv = v3(upad_p[vg], 258)[:, :, 1:257]
                cp("act" if vg == 0 else "dve", upv, v3(ps[:, 0:512], 256), [PSB[vg]], [B_up_p[vg]])
            for vg in range(2):
                ps = PS[2 + vg]
                fc = j + vg * NFC
                for ti_, tp in enumerate((3, 4, 5)):
                    dj = tp - 4
                    rhs = v3(upad_p[vg], 258)[:, :, 1 + dj:257 + dj]
                    mm(v3(ps[:, 0:512], 256), dgv[:, vg, tp, :], rhs, ti_ == 0, ti_ == 2, [B_dg[sl], B_up_p[vg]], [PSB[2 + vg]])
                bia = V("conv_b", layer * 44 + fc)
                k_ = it % 2
                if vg == 0:
                    act(vv[k_], ps[:, 0:512], AF.Identity, [PSB[2], B_vec], [B_vv[k_]], bias=bia)
                else:
                    act(sg[k_], ps[:, 0:512], AF.Silu, [PSB[3], B_vec], [B_sg[k_]], bias=bia)
            tt("dve", actv[:, j, 0:512], vv[it % 2], sg[it % 2], ALU.mult, [B_vv[it % 2], B_sg[it % 2]], [B_act])
            it += 1
            for vg in range(2):
                for (r0, nr) in s_tiles:
                    n = nr * 64
                    pb = 4 + (r0 // 8) % 2
                    ps = PS[pb]
                    t0 = 2 * LP + r0 * 64
                    for c in range(8):
                        mm(ps[:, 0:n], wv_[:, c, vg, :], h2v[:, c, t0:t0 + n], c == 0, c == 7, [B_wup[sl], B_h2], [PSB[pb]])
                    upv = v3(upad_s[vg], 66)[:, 1 + r0:1 + r0 + nr, 1:65]
                    cp("act" if (r0 // 8) % 2 == 0 else "dve", upv, v3(ps[:, 0:n], 64), [PSB[pb]], [B_up_s[vg]])
            for (r0, nr) in c_tiles:
                n = nr * 64
                t0 = 2 * LP + r0 * 64
                for vg in range(2):
                    ps = PS[6 + vg]
                    fc = j + vg * NFC
                    for tp in range(9):
                        di, dj = tp // 3 - 1, tp % 3 - 1
                        rhs = v3(upad_s[vg], 66)[:, 1 + r0 + di:1 + r0 + di + nr, 1 + dj:65 + dj]
                        mm(v3(ps[:, 0:n], 64), dgv[:, vg, tp, :], rhs, tp == 0, tp == 8, [B_dg[sl], B_up_s[vg]], [PSB[6 + vg]])
                    bia = V("conv_b", layer * 44 + fc)
                    k_ = it % 2
                    if vg == 0:
                        act(vv[k_][:, 0:n], ps[:, 0:n], AF.Identity, [PSB[6], B_vec], [B_vv[k_]], bias=bia)
                    else:
                        act(sg[k_][:, 0:n], ps[:, 0:n], AF.Silu, [PSB[7], B_vec], [B_sg[k_]], bias=bia)
                tt("dve", actv[:, j, t0:t0 + n], vv[it % 2][:, 0:n], sg[it % 2][:, 0:n], ALU.mult,
                   [B_vv[it % 2], B_sg[it % 2]], [B_act])
                it += 1
        A.release(m1)
        NDN = 2 * LP + n_s_dn
        wdn = [A.bf16(NFC * 128) for _ in range(2)]; B_wdn = [Buf("wdn0"), Buf("wdn1")]
        xall = A.f32(8 * NDN); B_xall = Buf("xall")
        xav = v3(xall, NDN)
        P.dma("sp", xav, s_x[:, :, 0:NDN], reads=[B_sx], writes=[B_xall], sembuf=B_xall)
        k = 0
        P.dma("pool", wdn[0], d_dnp[layer, 0], writes=[B_wdn[0]], sembuf=B_wdn[0])
        for dc in range(8):
            sl = dc % 2
            if dc + 1 < 8:
                P.dma("pool", wdn[1 - sl], d_dnp[layer, dc + 1], writes=[B_wdn[1 - sl]], sembuf=B_wdn[1 - sl])
            col = 0
            while col < NDN:
                n = min(512, NDN - col)
                cd = 0 if col < 2 * LP else 1
                pb = k % 4
                k += 1
                for fc in range(NFC):
                    mm(PS[pb][:, 0:n], v3(wdn[sl], 128)[:, fc, :], actv[:, fc, col:col + n], fc == 0, fc == NFC - 1,
                       [B_wdn[sl], B_act], [PSB[pb]])
                stt(xav[:, dc, col:col + n], PS[pb][:, 0:n], MOD(layer, 5, dc, cd), xav[:, dc, col:col + n], ALU.mult, ALU.add,
                    [PSB[pb], B_mod[layer], B_xall], [B_xall])
                col += n
        P.dma("sp", s_x[:, :, 0:NDN], xav, reads=[B_xall], writes=[B_sx], sembuf=B_xall)
        A.release(m0)
        P.barrier()


    def sgu_layer(layer, n_s):
        m0 = A.mark()
        NTK = 2 * LP + n_s
        win = A.bf16(8 * 2 * E2); B_win = Buf("sg_in")
        winv = v3(win, 2 * E2)
        for c in range(8):
            P.dma("pool", winv[:, c, :], d_sgin[c * 128:(c + 1) * 128, :], writes=[B_win], bulk="sgw")
        lnw = A.f32(E2); lnb = A.f32(E2); bs = A.f32(8); wsT = A.bf16(8 * 128); B_sgc = Buf("sgc")
        P.dma("sp", lnw, d_lnw, writes=[B_sgc], bulk="sgw")
        P.dma("sp", lnb, d_lnb, writes=[B_sgc], bulk="sgw")
        P.dma("sp", bs, d_bs, writes=[B_sgc], bulk="sgw")
        P.dma("pool", wsT, d_wsT, writes=[B_sgc], bulk="sgw")
        xgs = A.f32(8 * 512); B_xall = Buf("xg_s")
        xav_full = v3(xgs, 512)
        P.barrier()
        hg = A.bf16(8 * 512); B_hg = Buf("hg"); hgv = v3(hg, 512)
        nt = norm_tmps(512)
        class R_:
            pass

        def make_res(si):
            R = R_()
            R.u_f = A.bf16(E2); R.B_u = Buf("u_f%d" % si)
            R.v_f = A.f32(E2); R.B_v = Buf("v_f%d" % si)
            R.vln = A.bf16(E2); R.B_vln = Buf("vln%d" % si)
            R.gated = A.bf16(E2); R.B_gt = Buf("gated%d" % si)
            R.stats = A.f32(4 * 6); R.mv = A.f32(2); R.rs = A.f32(2); R.B_st = Buf("lnstats%d" % si)
            R.b = [4 * si + i for i in range(4)]
            R.si = si
            return R
        RS = [make_res(0), make_res(1)]
        gT = A.bf16(16 * 512); B_gT = [Buf("gT%d" % i) for i in range(4)]; gTv = v3(gT, 512)
        wso = [A.bf16(16 * 128) for _ in range(2)]; B_wso = [Buf("wso0"), Buf("wso1")]

        def chunk_gen(R, ck):
            u_f, v_f, vln, gated, stats, mv, rs = R.u_f, R.v_f, R.vln, R.gated, R.stats, R.mv, R.rs
            B_u, B_v, B_vln, B_gt, B_st = R.B_u, R.B_v, R.B_vln, R.B_gt, R.B_st
            bz = [R.b[0], R.b[1]]; bsp = R.b[2]; btr = R.b[3]
            for cb in range(8):
                pb = bz[cb % 2]
                for c in range(8):
                    mm(PS[pb][:, :], hgv[:, c, ck * 128:(ck + 1) * 128], winv[:, c, cb * 512:(cb + 1) * 512], c == 0, c == 7,
                       [B_hg, B_win], [PSB[pb]])
                yield
                if cb < 4:
                    act(u_f[:, cb * 512:(cb + 1) * 512], PS[pb][:, :], AF.Gelu, [PSB[pb]], [B_u])
                else:
                    act(v_f[:, (cb - 4) * 512:(cb - 3) * 512], PS[pb][:, :], AF.Gelu, [PSB[pb]], [B_v])
            yield
            for q in range(4):
                P.op("dve", lambda e, q=q: e.bn_stats(stats[:, q * 6:(q + 1) * 6], v_f[:, q * 512:(q + 1) * 512]), [B_v], [B_st])
            P.op("dve", lambda e: e.bn_aggr(mv, stats), [B_st], [B_st])
            yield
            act(rs[:, 0:1], mv[:, 1:2], AF.Ln, [B_st], [B_st], bias=LN_EPS)
            act(rs[:, 0:1], rs[:, 0:1], AF.Exp, [B_st], [B_st], scale=-0.5)
            yield
            stt(rs[:, 1:2], mv[:, 0:1], -1.0, rs[:, 0:1], ALU.mult, ALU.mult, [B_st], [B_st])
            ts("dve", v_f, v_f, rs[:, 0:1], ALU.mult, [B_v, B_st], [B_v], s2=rs[:, 1:2], op1=ALU.add)
            yield
            tt("dve", v_f, v_f, lnw, ALU.mult, [B_v, B_sgc], [B_v])
            yield
            tt("pool" if R.si == 0 else "dve", vln, v_f, lnb, ALU.add, [B_v, B_sgc], [B_vln])
            yield
            for g2_ in range(4):
                for gi in range(2):
                    g = g2_ * 2 + gi
                    mm(PS[bsp][:, gi * 256:(gi + 1) * 256], v3(wsT, 128)[:, g, :], vln[:, g * 256:(g + 1) * 256], True, True,
                       [B_sgc, B_vln], [PSB[bsp]])
                yield
                for gi in range(2):
                    g = g2_ * 2 + gi
                    stt(gated[:, g * 256:(g + 1) * 256], PS[bsp][:, gi * 256:(gi + 1) * 256], bs[:, g:g + 1],
                        u_f[:, g * 256:(g + 1) * 256], ALU.add, ALU.mult, [PSB[bsp], B_sgc, B_u], [B_gt])
                yield
            for h8 in range(2):
                pst = PS[btr].bitcast(BF16)
                for c8 in range(8):
                    cc = h8 * 8 + c8
                    tr(pst[:, c8 * 128:(c8 + 1) * 128], gated[:, cc * 128:(cc + 1) * 128], CB("ident"), [B_gt, B_cstb], [PSB[btr]])
                yield
                cp("act", gTv[:, h8 * 8:(h8 + 1) * 8, ck * 128:(ck + 1) * 128], v3(pst, 128), [PSB[btr]], [B_gT[ck]])
                yield

        def run_gens2(gens):
            alive = [True] * len(gens)
            while any(alive):
                for gi in range(len(gens)):
                    if alive[gi]:
                        try:
                            next(gens[gi])
                        except StopIteration:
                            alive[gi] = False
        col = 0
        kk_ = 0
        while col < NTK:
            n = min(512, NTK - col)
            cd = 0 if col < 2 * LP else 1
            xav = xav_full[:, :, 0:n]
            P.dma("sp", xav, s_x[:, :, col:col + n], reads=[B_sx], writes=[B_xall], sembuf=B_xall)
            rmsnorm_fm(xav, n, B_xall, lambda c, cd=cd: GM(layer, 0, c, cd), lambda c, cd=cd: MOD(layer, 0, c, cd),
                       lambda c, n=n: hgv[:, c, 0:n], B_hg, nt, psbank=7)
            nck = n // 128
            for ck0 in range(0, nck, 2):
                gens = [chunk_gen(RS[0], ck0)]
                if ck0 + 1 < nck:
                    gens.append(chunk_gen(RS[1], ck0 + 1))
                run_gens2(gens)
            for dc in range(8):
                sl = kk_ % 2
                kk_ += 1
                if kk_ == 1:
                    P.dma("pool", wso[0], d_sgop[0], writes=[B_wso[0]], sembuf=B_wso[0])
                P.dma("pool", wso[1 - sl], d_sgop[(dc + 1) % 8], writes=[B_wso[1 - sl]], sembuf=B_wso[1 - sl])
                pb = 6 + dc % 2
                for cc in range(16):
                    mm(PS[pb][:, 0:n], v3(wso[sl], 128)[:, cc, :], gTv[:, cc, 0:n], cc == 0, cc == 15, [B_wso[sl]] + B_gT, [PSB[pb]])
                stt(xav[:, dc, :], PS[pb][:, 0:n], MOD(layer, 2, dc, cd), xav[:, dc, :], ALU.mult, ALU.add,
                    [PSB[pb], B_mod[layer], B_xall], [B_xall])
            P.dma("sp", s_x[:, :, col:col + n], xav, reads=[B_xall], writes=[B_sx], sembuf=B_xall)
            col += n
        A.release(m0)
        P.barrier()


    def final_out():
        m0 = A.mark()
        NTK = 2 * LP + OUTS
        xg2 = [A.f32(8 * 512) for _ in range(2)]; Bxg2 = [Buf("xg_o0"), Buf("xg_o1")]
        yg = A.f32(8 * 512); Byg = Buf("yg_o"); ygv = v3(yg, 512)
        ot = [A.f32(D) for _ in range(2)]; Bot = [Buf("ot0"), Buf("ot1")]
        nt = norm_tmps(512)
        B_o = Buf("outs")
        col = 0
        k = 0
        npc = NTK // 512
        P.dma("sp", v3(xg2[0], 512), s_x[:, :, 0:512], reads=[B_sx], writes=[Bxg2[0]], sembuf=Bxg2[0])
        pi = 0
        while col < NTK:
            n = 512
            if pi + 1 < npc:
                P.dma("sp", v3(xg2[(pi + 1) % 2], 512), s_x[:, :, col + 512:col + 1024], reads=[B_sx], writes=[Bxg2[(pi + 1) % 2]],
                      sembuf=Bxg2[(pi + 1) % 2])
            xgv = v3(xg2[pi % 2], 512); Bxg = Bxg2[pi % 2]
            pi += 1
            rmsnorm_fm(xgv, n, Bxg, lambda c: V("norm_final", c), lambda c: None, lambda c: ygv[:, c, :], Byg, nt)
            for t4 in range(4):
                sl = k % 2
                k += 1
                for c in range(8):
                    pb = 4 + 2 * (c // 4) + (t4 % 2)
                    tr(PS[pb][:, (c % 4) * 128:(c % 4 + 1) * 128], ygv[:, c, t4 * 128:(t4 + 1) * 128], C("ident"), [Byg, B_cst], [PSB[pb]])
                cp("act", ot[sl][:, 0:512], PS[4 + (t4 % 2)][:, :], [PSB[4 + (t4 % 2)]], [Bot[sl]])
                cp("dve", ot[sl][:, 512:1024], PS[6 + (t4 % 2)][:, :], [PSB[6 + (t4 % 2)]], [Bot[sl]])
                tok = col + t4 * 128
                if tok < 2 * LP:
                    dst = o_yp[tok:tok + 128, :]
                else:
                    dst = o_ys[tok - 2 * LP:tok - 2 * LP + 128, :]
                P.dma("sp", dst, ot[sl], reads=[Bot[sl]], writes=[B_o], sembuf=Bot[sl])
            col += n
        A.release(m0)

    try:
        compute_mod(0)
        chk("mod0")
        phase0()
        chk("phase0")
        rwkv_layer()
        A.release(m_rwkv)
        P.barrier()
        chk("rwkv")
        compute_mod(1)
        ffn_layer(0, WIN, WIN - 128)
        chk("ffn0")
        sgu_layer(1, WIN - 128)
        chk("sgu")
        ffn_layer(1, WIN - 128, OUTS)
        chk("ffn1")
        final_out()
    except StopBuild:
        pass
    P.barrier()
    P.final_wait("sp")
    K.arena_hi = A.hi
    P.emit(ctx)
    ctx.close()
    return nc


def _fm(vec_1024):
    v = np.asarray(vec_1024, np.float32)
    lead = v.shape[:-1]
    n = v.shape[-1] // 128
    v = v.reshape(lead + (n, 128))
    return np.moveaxis(v, -1, 0)


def prep_inputs(x_prompt, x_sample, state_ctx_fwd, state_ctx_bwd, c, c_ctx,
           ada_w, ada_b, norm_mix, norm_ffn, ffn_up, ffn_conv, ffn_conv_b, ffn_down, norm_final,
           rw_mu, rw_wr, rw_wk, rw_wv, rw_wo, rw_w0, rw_w1, rw_w2, rw_a0, rw_a1, rw_a2,
           rw_g1, rw_g2, rw_kk, rw_ka, rw_rk, rw_lnx_w, rw_lnx_b,
           sg_in, sg_ln_w, sg_ln_b, sg_ws, sg_bs, sg_out):
    f = lambda a: np.ascontiguousarray(np.asarray(a, np.float32))
    x_prompt, x_sample = f(x_prompt), f(x_sample)
    consts = make_consts()
    def kcp(w):
        w = np.asarray(w, np.float32)
        return np.transpose(w.reshape(8, 128, w.shape[1]), (1, 0, 2))
    wr_, wk_, wv_ = (kcp(np.asarray(a)[0]) for a in (rw_wr, rw_wk, rw_wv))
    wkvp = np.stack([np.stack([w[:, :, hp * 128:(hp + 1) * 128] for w in (wr_, wk_, wv_)], 2) for hp in range(NHP)], 0)
    wkvp = f(wkvp.reshape(NHP, 128, 8 * 3 * 128))
    wo_ = np.transpose(np.asarray(rw_wo, np.float32)[0].reshape(8, 128, D), (1, 0, 2))
    wop = f(np.stack([wo_[:, :, dc * 128:(dc + 1) * 128] for dc in range(8)], 0).reshape(8, 128, 8 * 128))
    upp = np.zeros((2, NFC, 128, 8, 2, 128), np.float32)
    dnp = np.zeros((2, 8, 128, NFC, 128), np.float32)
    for l in range(2):
        u = kcp(np.asarray(ffn_up)[l])
        dn = np.transpose(np.asarray(ffn_down, np.float32)[l].reshape(NFC, 128, D), (1, 0, 2))
        for j in range(NFC):
            upp[l, j, :, :, 0, :] = u[:, :, j * 128:(j + 1) * 128]
            upp[l, j, :, :, 1, :] = u[:, :, F + j * 128:F + (j + 1) * 128]
        for dc in range(8):
            dnp[l, dc] = dn[:, :, dc * 128:(dc + 1) * 128]
    upp = f(upp.reshape(2, NFC, 128, 8 * 2 * 128))
    dnp = f(dnp.reshape(2, 8, 128, NFC * 128))
    so_ = np.transpose(np.asarray(sg_out, np.float32)[0].reshape(16, 128, D), (1, 0, 2))
    sgop = f(np.stack([so_[:, :, dc * 128:(dc + 1) * 128] for dc in range(8)], 0).reshape(8, 128, 16 * 128))
    in_maps = []
    for core in range(8):
        odd = core % 2 == 1
        b = core // 2
        xp = x_prompt[2 * core:2 * core + 2]
        xs = x_sample[b]
        if odd:
            xp = xp[:, ::-1]
            xs = xs[::-1]
        sts = (state_ctx_bwd, state_ctx_fwd) if odd else (state_ctx_fwd, state_ctx_bwd)
        st = np.zeros((2, NHP, 128, 64), np.float32)
        for d in range(2):
            S = np.asarray(sts[d], np.float32)[b, 0]
            Ht = np.transpose(S, (0, 2, 1))
            st[d] = Ht.reshape(NHP, 128, 64)
        cond = np.zeros((128, 8, 2), np.float32)
        cond[:, :, 0] = _fm(c_ctx)
        cond[:, :, 1] = _fm(np.asarray(c)[b])
        vecs = np.zeros((128, NV), np.float32)

        def put(name, arr):
            o, n = VEC[name]
            vecs[:, o:o + n] = np.asarray(arr, np.float32).reshape(128, n)
        put("ada_b", _fm(np.asarray(ada_b).reshape(2, 6 * D)).reshape(128, 96))
        put("norm_mix", _fm(norm_mix).reshape(128, 16))
        put("norm_ffn", _fm(norm_ffn).reshape(128, 16))
        put("norm_final", _fm(norm_final))
        put("conv_b", _fm(ffn_conv_b).reshape(128, 88))
        cw = np.asarray(ffn_conv, np.float32)
        if odd:
            cw = cw[:, ::-1, ::-1]
        cw = cw.reshape(2, 9, 2 * F)
        cwf = _fm(cw)
        put("conv_w", np.transpose(cwf, (0, 1, 3, 2)).reshape(128, 2 * 44 * 9))
        mu = np.asarray(rw_mu, np.float32)[0]
        if odd:
            mu = mu[::-1]
        muf = _fm(mu)
        put("mu", np.transpose(muf, (0, 3, 1, 2)).reshape(128, 96))
        dsel = [1, 0] if odd else [0, 1]
        put("w0", _fm(np.asarray(rw_w0)[0][dsel]).reshape(128, 16))
        put("a0", _fm(np.asarray(rw_a0)[0][dsel]).reshape(128, 16))
        put("k_k", _fm(np.asarray(rw_kk)[0]))
        put("k_a", _fm(np.asarray(rw_ka)[0]))
        put("r_k", _fm(np.asarray(rw_rk)[0].reshape(D)))
        put("lnx_w", _fm(np.asarray(rw_lnx_w)[0]))
        put("lnx_b", _fm(np.asarray(rw_lnx_b)[0]))
        ws = np.asarray(sg_ws, np.float32)[0]
        bsv = np.asarray(sg_bs, np.float32)[0]
        if odd:
            ws = ws[:, ::-1, ::-1]
            bsv = bsv[:, ::-1]
        wsT = np.transpose(ws, (2, 0, 1)).reshape(128, 8 * 128)
        m = {
            "xp": f(xp.reshape(2 * LP, D)), "xs": f(xs), "st": st, "cond": f(cond.reshape(128, 16)),
            "vec": vecs, "cst": consts,
            "ada_w": f(ada_w), "ffn_upp": upp, "ffn_dnp": dnp, "wkvp": wkvp, "wop": wop,
            "w1": f(np.asarray(rw_w1)[0][dsel]), "w2": f(np.asarray(rw_w2)[0][dsel]),
            "a1": f(np.asarray(rw_a1)[0][dsel]), "a2": f(np.asarray(rw_a2)[0][dsel]),
            "g1": f(np.asarray(rw_g1)[0]), "g2": f(np.asarray(rw_g2)[0]),
            "sg_in": f(np.asarray(sg_in)[0]), "sg_outp": sgop,
            "sg_lnw": f(np.broadcast_to(np.asarray(sg_ln_w, np.float32)[0][None, :], (128, E2))),
            "sg_lnb": f(np.broadcast_to(np.asarray(sg_ln_b, np.float32)[0][None, :], (128, E2))),
            "sg_bs": f(bsv.T), "sg_wsT": f(wsT),
        }
        in_maps.append(m)
    return in_maps


def assemble(results):
    y_prompt = np.zeros((16, LP, D), np.float32)
    y_sample = np.zeros((4, LS, D), np.float32)
    nsf = np.zeros((16, 1, 16, 64, 64), np.float32)
    nsb = np.zeros((16, 1, 16, 64, 64), np.float32)
    for core in range(8):
        r = results[core]
        odd = core % 2 == 1
        b = core // 2
        yp = np.asarray(r["yp"], np.float32).reshape(2, LP, D)
        ys = np.asarray(r["ys"], np.float32)
        ns = np.asarray(r["ns"], np.float32)
        if odd:
            yp = yp[:, ::-1]
            y_sample[b, LS - OUTS:] = ys[::-1]
        else:
            y_sample[b, :OUTS] = ys
        y_prompt[2 * core:2 * core + 2] = yp
        for si in range(2):
            for d in range(2):
                S = np.transpose(ns[si, d].reshape(16, 64, 64), (0, 2, 1))
                tgt_fwd = (d == 0) != odd
                if tgt_fwd:
                    nsf[2 * core + si, 0] = S
                else:
                    nsb[2 * core + si, 0] = S
    return (y_prompt, y_sample, nsf, nsb)


def kernel(**inputs):
    in_maps = prep_inputs(**inputs)
    nc = build_program()
    res = run_bass_kernel_spmd(nc, in_maps, core_ids=list(range(8)))
    return assemble(res.results)
```

```python
import numpy as np
from contextlib import ExitStack
import concourse.bass as bass
import concourse.mybir as mybir
from concourse.bass_utils import run_bass_kernel_spmd

F32 = mybir.dt.float32
BF16 = mybir.dt.bfloat16
AF = mybir.ActivationFunctionType
ALU = mybir.AluOpType

D = 1024
NCH = 8
NHP = 8
LP = 256
LS = 2048
WIN = 1280
OUTS = 1024
TT = 2 * LP + WIN
XB_P = [0, LP]
XB_S = 2 * LP
HB_P = [1, LP + 3]
HB_S = 2 * (LP + 2) + 1
HL = 2 * (LP + 2) + LS + 2
F = 2816
NFC = 22
E2 = 2048
CDEC = 0.6065306597126334
RMS_EPS = 1e-6
LN_EPS = 1e-5
GN_EPS = 64e-5
NB = 256

ENGS = ["pe", "act", "dve", "pool", "sp"]
SAME_SYNC = {"pe": False, "act": True, "dve": True, "pool": True, "sp": False}


class Buf:
    __slots__ = ("name", "lw", "rd", "dsem", "dcnt", "excl", "pe_rows")

    def __init__(self, name, excl=False):
        self.name = name
        self.lw = None
        self.rd = {}
        self.dsem = None
        self.dcnt = 0
        self.excl = excl
        self.pe_rows = None


class Prog:
    def __init__(self, nc):
        self.nc = nc
        self.ops = {e: [] for e in ENGS}
        self.cnt = {e: 0 for e in ENGS}
        self.waited = {e: {} for e in ENGS}
        self.sem_objs = {}
        self.dma_keys = []
        self.bulk = {}
        self.sembufs = []
        self.inflight = {"sp": [], "pool": []}
        self.max_inflight = {"sp": 8, "pool": 2}

    def _need(self, eng, deps, key, val):
        if key == eng and not SAME_SYNC[eng]:
            return
        if self.waited[eng].get(key, 0) >= val:
            return
        if deps.get(key, 0) < val:
            deps[key] = val

    def _collect(self, eng, reads, writes):
        deps = {}
        for b in reads:
            if b.lw is not None:
                self._need(eng, deps, *b.lw)
        for b in writes:
            if b.lw is not None:
                self._need(eng, deps, *b.lw)
            for k, v in b.rd.items():
                self._need(eng, deps, k, v)
        for k, v in deps.items():
            self.waited[eng][k] = v
        return list(deps.items())

    def op(self, eng, fn, reads=(), writes=(), rows=None):
        ex = [b for b in reads if b.excl]
        if ex:
            reads = [b for b in reads if not b.excl]
            writes = list(writes) + ex
        waits = self._collect(eng, reads, writes)
        if eng == "pe" and rows is not None:
            for b in writes:
                if b.excl and b.pe_rows is not None:
                    r0, n0, sq = b.pe_rows
                    if (r0 + n0 <= rows[0] or rows[0] + rows[1] <= r0) and self.waited["pe"].get("pe", 0) < sq:
                        self.waited["pe"]["pe"] = sq
                        waits = waits + [("pe", sq)]
        self.cnt[eng] += 1
        seq = self.cnt[eng]
        if eng == "pe" and rows is not None:
            for b in writes:
                if b.excl:
                    b.pe_rows = (rows[0], rows[1], seq)
        self.ops[eng].append((waits, fn, (eng, 1)))
        for b in reads:
            b.rd[eng] = seq
        for b in writes:
            b.lw = (eng, seq)
            b.rd = {}
        return seq

    def dma(self, q, out_ap, in_ap, reads=(), writes=(), sembuf=None, bulk=None):
        waits = self._collect(q, reads, writes)
        if bulk is not None:
            key = ("bulk", bulk, q)
            self.bulk[key] = self.bulk.get(key, 0) + 16
            val = self.bulk[key]
        else:
            if sembuf.dsem is None:
                sembuf.dsem = {}
                sembuf.dcnt = {}
                self.sembufs.append(sembuf)
            if q not in sembuf.dsem:
                sembuf.dsem[q] = ("dma", len(self.dma_keys), q)
                sembuf.dcnt[q] = 0
                self.dma_keys.append(sembuf.dsem[q])
            sembuf.dcnt[q] += 16
            key = sembuf.dsem[q]
            val = sembuf.dcnt[q]

        def fn(e, out_ap=out_ap, in_ap=in_ap):
            return e.dma_start(out=out_ap, in_=in_ap)
        fl = self.inflight[q]
        while len(fl) >= self.max_inflight[q]:
            k0, v0 = fl.pop(0)
            if self.waited[q].get(k0, 0) < v0:
                self.waited[q][k0] = v0
                waits = waits + [(k0, v0)]
        fl.append((key, val))
        self.ops[q].append((waits, fn, (key, 16)))
        for b in reads:
            b.rd[key] = max(b.rd.get(key, 0), val)
        for b in writes:
            b.lw = (key, val)
            b.rd = {}

    def barrier(self):
        for e in ENGS:
            deps = {}
            for f in ENGS:
                if f != e and self.cnt[f] > 0:
                    self._need(e, deps, f, self.cnt[f])
            for k, v in self.bulk.items():
                self._need(e, deps, k, v)
            for b in self.sembufs:
                for q, k in b.dsem.items():
                    self._need(e, deps, k, b.dcnt[q])
            for k, v in deps.items():
                self.waited[e][k] = v
            if deps:
                self.ops[e].append((list(deps.items()), None, None))

    def final_wait(self, eng="sp"):
        waits = [(k, v) for k, v in self.bulk.items()]
        for b in self.sembufs:
            for q, k in b.dsem.items():
                waits.append((k, b.dcnt[q]))
        self.ops[eng].append((waits, None, None))

    def emit(self, ctx):
        nc = self.nc
        keys = list(ENGS) + self.dma_keys + list(self.bulk.keys())
        for k in keys:
            nm = "s_" + "_".join(str(x) for x in (k if isinstance(k, tuple) else (k,)))
            self.sem_objs[k] = ctx.enter_context(nc.semaphore(nm))
        block = ctx.enter_context(nc.Block())

        def run(e, name):
            for waits, fn, inc in self.ops[name]:
                for k, v in waits:
                    e.wait_ge(self.sem_objs[k], v)
                if fn is not None:
                    ins = fn(e)
                    ins.then_inc(self.sem_objs[inc[0]], inc[1])
        block.tensor(lambda e: run(e, "pe"))
        block.scalar(lambda e: run(e, "act"))
        block.vector(lambda e: run(e, "dve"))
        block.gpsimd(lambda e: run(e, "pool"))
        block.sync(lambda e: run(e, "sp"))


VEC = {}
_off = 0


def _vreg(name, n):
    global _off
    VEC[name] = (_off, n)
    _off += n


_vreg("ada_b", 2 * 48)
_vreg("norm_mix", 2 * 8)
_vreg("norm_ffn", 2 * 8)
_vreg("norm_final", 8)
_vreg("conv_b", 2 * 44)
_vreg("conv_w", 2 * 44 * 9)
_vreg("mu", 8 * 12)
_vreg("w0", 2 * 8)
_vreg("a0", 2 * 8)
_vreg("k_k", 8)
_vreg("k_a", 8)
_vreg("r_k", 8)
_vreg("lnx_w", 8)
_vreg("lnx_b", 8)
NV = _off

CSTF = {"ident": (0, 128), "maskc": (128, 256)}
NCF = 384
CSTB = {}
_coff = 0


def _creg(name, n):
    global _coff
    CSTB[name] = (_coff, n)
    _coff += n


_creg("ident", 128)
_creg("blk1", 128)
_creg("blk64", 128)
_creg("ones1024", 128)
_creg("id2", 256)
_creg("mt0", 1024)
_creg("mt1", 1024)
_creg("mb0", 256)
_creg("mb1", 256)
NCB = _coff


def make_consts():
    c = np.zeros((128, NCF + NCB), np.float32)
    s = np.arange(128)[:, None]
    t = np.arange(128)[None, :]
    same = (s // 64) == (t // 64)
    c[:, 0:128] = np.eye(128)
    c[:, 128:384] = (np.arange(256) % 64 != 0)[None, :]

    def putb(name, arr):
        o, n = CSTB[name]
        c[:, NCF + o:NCF + o + n] = arr
    putb("ident", np.eye(128))
    putb("blk1", same)
    putb("blk64", same / 64.0)
    putb("ones1024", np.full((128, 128), 1.0 / 1024.0))
    putb("id2", np.concatenate([np.eye(128), np.eye(128)], 1))
    st0 = same & (s < t); in0 = same & (s <= t)
    st1 = same & (s > t); in1 = same & (s >= t)
    m0 = np.concatenate([st0, in0, st0, in0], 1)
    m1 = np.concatenate([st1, in1, st1, in1], 1)
    putb("mt0", np.concatenate([m0, m0], 1))
    putb("mt1", np.concatenate([m1, m1], 1))
    b0 = same & (t < s)
    b1 = same & (t > s)
    putb("mb0", np.concatenate([b0, b0], 1))
    putb("mb1", np.concatenate([b1, b1], 1))
    return c


class K:
    stop = None


class StopBuild(Exception):
    pass


def chk(name):
    if K.stop is not None and K.stop == name:
        raise StopBuild(name)


def build_program(debug=False):
    nc = bass.Bass("TRN2", target_bir_lowering=False)

    def din(name, shape):
        return nc.dram_tensor(name, list(shape), F32, kind="ExternalInput").ap()

    def dout(name, shape):
        return nc.dram_tensor(name, list(shape), F32, kind="ExternalOutput").ap()

    d_xp = din("xp", [2 * LP, D])
    d_xs = din("xs", [LS, D])
    d_st = din("st", [2, NHP, 128, 64])
    d_cond = din("cond", [128, 16])
    d_vec = din("vec", [128, NV])
    d_cst = din("cst", [128, NCF + NCB])
    d_ada = din("ada_w", [2, D, 6 * D])
    d_upp = din("ffn_upp", [2, NFC, 128, 8 * 2 * 128])
    d_dnp = din("ffn_dnp", [2, 8, 128, NFC * 128])
    d_wkvp = din("wkvp", [NHP, 128, 8 * 3 * 128])
    d_wop = din("wop", [8, 128, 8 * 128])
    d_w1 = din("w1", [2, D, 64]); d_w2 = din("w2", [2, 64, D])
    d_a1 = din("a1", [2, D, 64]); d_a2 = din("a2", [2, 64, D])
    d_g1 = din("g1", [D, 128]); d_g2 = din("g2", [128, D])
    d_sgin = din("sg_in", [D, 2 * E2]); d_sgop = din("sg_outp", [8, 128, 16 * 128])
    d_lnw = din("sg_lnw", [128, E2]); d_lnb = din("sg_lnb", [128, E2])
    d_bs = din("sg_bs", [128, 8]); d_wsT = din("sg_wsT", [128, 8 * 128])
    o_yp = dout("yp", [2 * LP, D])
    o_ys = dout("ys", [OUTS, D])
    o_ns = dout("ns", [2, 2, NHP, 128, 64])
    s_x = nc.dram_tensor("s_x", [128, NCH, TT], F32, kind="Internal").ap()
    s_y0 = nc.dram_tensor("s_y0", [2, NHP, 128, WIN], F32, kind="Internal").ap()

    ctx = ExitStack()
    AW = 52800
    arena = ctx.enter_context(nc.sbuf_tensor("arena", [128, AW], F32))
    PS = [ctx.enter_context(nc.psum_tensor("psb%d" % i, [128, 512], F32)) for i in range(8)]
    P = Prog(nc)

    class Arena:
        def __init__(self):
            self.top = 0
            self.hi = 0

        def f32(self, n):
            o = self.top
            self.top += n
            self.hi = max(self.hi, self.top)
            assert self.top <= AW, "SBUF arena overflow %d" % self.top
            return arena[:, o:o + n]

        def bf16(self, n):
            n2 = (n + 1) // 2
            return self.f32(n2).bitcast(BF16)[:, 0:n]

        def mark(self):
            return self.top

        def release(self, m):
            self.top = m
            P.barrier()
    A = Arena()

    def v3(ap, b):
        return ap.rearrange("p (a b) -> p a b", b=b)

    def v4(ap, b, c):
        return ap.rearrange("p (a b c) -> p a b c", b=b, c=c)

    def mm(out, lhsT, rhs, start, stop, reads, writes):
        rows = (int(lhsT.start_partition()), int(lhsT.partition_size()))
        P.op("pe", lambda e: e.matmul(out, lhsT, rhs, start=start, stop=stop), reads, writes, rows=rows)

    def tr(out, in_, ident, reads, writes):
        rows = (int(in_.start_partition()), int(in_.partition_size()))
        P.op("pe", lambda e: e.transpose(out, in_, ident), reads, writes, rows=rows)

    def act(out, in_, func, reads, writes, bias=None, scale=None):
        kw = {}
        if bias is not None:
            kw["bias"] = bias
        if scale is not None:
            kw["scale"] = scale
        P.op("act", lambda e: e.activation(out=out, in_=in_, func=func, **kw), reads, writes)

    def tt(eng, out, in0, in1, op, reads, writes):
        P.op(eng, lambda e: e.tensor_tensor(out, in0, in1, op), reads, writes)

    def ts(eng, out, in0, s1, op0, reads, writes, s2=None, op1=None):
        if op1 is None:
            P.op(eng, lambda e: e.tensor_scalar(out, in0, s1, None, op0), reads, writes)
        else:
            P.op(eng, lambda e: e.tensor_scalar(out, in0, s1, s2, op0, op1), reads, writes)

    def stt(out, in0, scalar, in1, op0, op1, reads, writes):
        P.op("dve", lambda e: e.scalar_tensor_tensor(out, in0, scalar, in1, op0, op1), reads, writes)

    def cp(eng, out, in_, reads, writes):
        if eng == "act":
            P.op("act", lambda e: e.copy(out, in_), reads, writes)
        else:
            P.op(eng, lambda e: e.tensor_copy(out, in_), reads, writes)

    def memset(eng, ap, val, writes):
        P.op(eng, lambda e: e.memset(ap, val), (), writes)

    cst = A.f32(NCF); B_cst = Buf("cst")
    vec = A.f32(NV); B_vec = Buf("vec")
    cstb = A.bf16(NCB); B_cstb = Buf("cstb")
    Hs0 = A.f32(2 * NHP * 64); B_Hs0 = Buf("Hs0")
    cond = A.f32(16); B_cond = Buf("cond")
    scond = A.bf16(16); B_scond = Buf("scond")
    mod = A.f32(2 * 96); B_mod = [Buf("mod0"), Buf("mod1")]
    gm = A.f32(2 * 2 * 16); B_gm = Buf("gm")

    def V(name, i=0, n=1):
        o, _ = VEC[name]
        return vec[:, o + i:o + i + n]

    def C(name, a=0, b=None):
        o, n = CSTF[name]
        return cst[:, o + a:o + (n if b is None else b)]

    def CB(name, a=0, b=None):
        o, n = CSTB[name]
        return cstb[:, o + a:o + (n if b is None else b)]

    def MOD(layer, m, ch, cd):
        o = layer * 96 + (m * 8 + ch) * 2 + cd
        return mod[:, o:o + 1]

    def GM(layer, which, ch, cd):
        o = layer * 32 + which * 16 + ch * 2 + cd
        return gm[:, o:o + 1]

    P.dma("sp", cst, d_cst[:, 0:NCF], writes=[B_cst], bulk="c")
    P.dma("sp", vec, d_vec, writes=[B_vec], bulk="c")
    P.dma("sp", cond, d_cond, writes=[B_cond], bulk="c")
    P.dma("sp", Hs0.rearrange("p (a i) -> p a i", i=64), d_st.rearrange("d h p i -> p (d h) i"), writes=[B_Hs0], bulk="c")
    P.dma("pool", cstb, d_cst[:, NCF:NCF + NCB], writes=[B_cstb], bulk="c")
    P.barrier()
    act(scond, cond, AF.Silu, [B_cond], [B_scond])

    PSB = [Buf("psb%d" % i, excl=True) for i in range(8)]

    def compute_mod(layer):
        m0 = A.mark()
        wb = [A.bf16(8 * 768) for _ in range(2)]
        Bw = [Buf("adaw0"), Buf("adaw1")]
        psm = PS[7][:, 0:96]
        for pc in range(8):
            sl = pc % 2
            src = d_ada[layer].rearrange("(kc p) f -> p kc f", p=128)[:, :, pc * 768:(pc + 1) * 768]
            P.dma("pool", v3(wb[sl], 768), src, writes=[Bw[sl]], sembuf=Bw[sl])
            wv_ = v3(wb[sl], 768)
            for fch in range(6):
                col = (pc * 6 + fch) * 2
                for kc in range(8):
                    mm(psm[:, col:col + 2], wv_[:, kc, fch * 128:(fch + 1) * 128], scond[:, kc * 2:kc * 2 + 2],
                       kc == 0, kc == 7, [Bw[sl], B_scond], [PSB[7]])
        modv = v3(mod[:, layer * 96:(layer + 1) * 96], 2)
        psv = v3(psm, 2)
        ab = V("ada_b", layer * 48, 48)
        for cd in range(2):
            tt("dve", modv[:, :, cd], psv[:, :, cd], ab, ALU.add, [PSB[7], B_vec], [B_mod[layer]])
        for which, (nm, msc) in enumerate((("norm_mix", 1), ("norm_ffn", 4))):
            for cd in range(2):
                o = layer * 32 + which * 16
                gv = v3(gm[:, o:o + 16], 2)[:, :, cd]
                scv = v3(mod[:, layer * 96 + msc * 16: layer * 96 + msc * 16 + 16], 2)[:, :, cd]
                ts("dve", gv, scv, 1.0, ALU.add, [B_mod[layer]], [B_gm])
                tt("dve", gv, gv, V(nm, layer * 8, 8), ALU.mult, [B_gm, B_vec], [B_gm])
        A.release(m0)

    def rmsnorm_fm(x_ap, n, Bx, scale_fn, bias_fn, out_fn, Bout, tmpbufs, psbank=6, extra_reads=()):
        sq, Bsq, rstd, Brs, tmp, Btmp = tmpbufs
        P.op("act", lambda e: e.activation(out=v3(sq[:, 0:8 * n], n), in_=x_ap, func=AF.Square), [Bx], [Bsq])
        ps = PS[psbank][:, 0:n]
        for c in range(8):
            mm(ps, CB("ones1024"), sq[:, c * n:(c + 1) * n], c == 0, c == 7, [Bsq, B_cstb], [PSB[psbank]])
        act(rstd[:, 0:n], ps, AF.Ln, [PSB[psbank]], [Brs], bias=RMS_EPS)
        act(rstd[:, 0:n], rstd[:, 0:n], AF.Exp, [Brs], [Brs], scale=-0.5)
        for c in range(8):
            eng = "dve" if c % 2 == 0 else "pool"
            tsl = tmp[c % 2][:, 0:n]
            tt(eng, tsl, x_ap[:, c, :], rstd[:, 0:n], ALU.mult, [Bx, Brs], [Btmp[c % 2]])
            b = bias_fn(c)
            if b is None:
                act(out_fn(c), tsl, AF.Copy, [Btmp[c % 2], B_gm, B_vec] + list(extra_reads), [Bout], scale=scale_fn(c))
            else:
                act(out_fn(c), tsl, AF.Identity, [Btmp[c % 2], B_gm, B_mod[0], B_mod[1], B_vec] + list(extra_reads),
                    [Bout], bias=b, scale=scale_fn(c))

    def norm_tmps(n):
        sq = A.bf16(8 * n)
        rstd = A.f32(n)
        tmp = [A.f32(n), A.f32(n)]
        return (sq, Buf("nsq"), rstd, Buf("nrstd"), tmp, [Buf("ntmp0"), Buf("ntmp1")])

    m_rwkv = A.mark()
    hpad = A.bf16(8 * HL); B_h = Buf("hpad")
    hv = v3(hpad, HL)
    P.op("pool", lambda e: e.memset(hpad, 0.0), (), [B_h])
    B_sx = Buf("s_x")
    B_sy0 = Buf("s_y0")

    def phase0():
        m0 = A.mark()
        xt = [A.f32(D) for _ in range(2)]
        Bxt = [Buf("xt0"), Buf("xt1")]
        xg = A.f32(8 * 512); Bxg = Buf("xg")
        xgv = v3(xg, 512)
        nt = norm_tmps(512)
        groups = [(d_xp, 0, [(HB_P[0], 256), (HB_P[1], 256)], 0, 0)]
        for g in range(4):
            groups.append((d_xs, g * 512, [(HB_S + g * 512, 512)], 1, (XB_S + g * 512) if g * 512 < WIN else None))
        ti = 0
        for (src, row0, pieces, cd, xb) in groups:
            for t4 in range(4):
                sl = ti % 2
                ti += 1
                P.dma("sp", xt[sl], src[row0 + t4 * 128: row0 + (t4 + 1) * 128, :], writes=[Bxt[sl]], sembuf=Bxt[sl])
                pb = 4 + (t4 % 2)
                for c in range(8):
                    bk = pb if c < 4 else pb + 2
                    tr(PS[bk][:, (c % 4) * 128:(c % 4 + 1) * 128], xt[sl][:, c * 128:(c + 1) * 128], C("ident"),
                       [Bxt[sl], B_cst], [PSB[bk]])
                for half in range(2):
                    bank = pb + 2 * half
                    eng = "act" if half == 0 else "dve"
                    cp(eng, xgv[:, half * 4:(half + 1) * 4, t4 * 128:(t4 + 1) * 128], v3(PS[bank][:, :], 128), [PSB[bank]], [Bxg])
            if xb is not None:
                nvalid = min(512, TT - xb)
                P.dma("sp", s_x[:, :, xb:xb + nvalid], xgv[:, :, 0:nvalid], reads=[Bxg], writes=[B_sx], bulk="sx")
            col = 0
            for (hb, ntok) in pieces:
                rmsnorm_fm(xgv[:, :, col:col + ntok], ntok, Bxg,
                           lambda c, cd=cd: GM(0, 0, c, cd), lambda c, cd=cd: MOD(0, 0, c, cd),
                           lambda c, hb=hb, ntok=ntok: hv[:, c, hb:hb + ntok], B_h, nt)
                col += ntok
        A.release(m0)

    def rwkv_layer():
        m0 = A.mark()
        B_w = Buf("rwkv_w")
        w1 = [A.bf16(8 * 64) for _ in range(2)]; a1 = [A.bf16(8 * 64) for _ in range(2)]
        w2 = [A.bf16(D) for _ in range(2)]; a2 = [A.bf16(D) for _ in range(2)]
        g1 = A.bf16(8 * 128); g2 = A.bf16(D)
        for d in range(2):
            P.dma("pool", v3(w1[d], 64), d_w1[d].rearrange("(kc p) f -> p kc f", p=128), writes=[B_w], bulk="rw")
            P.dma("pool", v3(a1[d], 64), d_a1[d].rearrange("(kc p) f -> p kc f", p=128), writes=[B_w], bulk="rw")
            P.dma("pool", w2[d][0:64, :], d_w2[d], writes=[B_w], bulk="rw")
            P.dma("pool", a2[d][0:64, :], d_a2[d], writes=[B_w], bulk="rw")
        P.dma("pool", v3(g1, 128), d_g1.rearrange("(kc p) f -> p kc f", p=128), writes=[B_w], bulk="rw")
        P.dma("pool", g2, d_g2, writes=[B_w], bulk="rw")
        wob = [A.bf16(8 * 128) for _ in range(2)]; B_wob = [Buf("wob0"), Buf("wob1")]
        mt2 = [CB("mt0"), CB("mt1")]; mb2 = [CB("mb0"), CB("mb1")]; id2 = CB("id2")
        Hf = A.f32(2 * NHP * 64); B_Hf = [[Buf("Hf%d_%d" % (d, h)) for h in range(NHP)] for d in range(2)]
        Hb = A.bf16(2 * 2 * NHP * 64)
        B_Hb = [[[Buf("Hb%d_%d_%d" % (i, d, h)) for h in range(NHP)] for d in range(2)] for i in range(2)]
        hb_par = [[0] * NHP for _ in range(2)]

        def HF(d, hp):
            o = (d * NHP + hp) * 64
            return Hf[:, o:o + 64]

        def HS0(d, hp):
            o = (d * NHP + hp) * 64
            return Hs0[:, o:o + 64]

        def HB(i, d, hp):
            o = ((i * 2 + d) * NHP + hp) * 64
            return Hb[:, o:o + 64]
        xs3 = {m: A.bf16(8 * NB) for m in ("r", "k", "v")}
        B_xs = {m: Buf("xs_" + m) for m in ("r", "k", "v", "t")}
        xst = A.bf16(8 * NB)
        dd1 = A.bf16(8 * NB); dd2 = A.bf16(8 * NB); B_dd = Buf("dd")
        tmpx = [A.f32(NB), A.f32(NB)]; B_tmpx = [Buf("tmpx0"), Buf("tmpx1")]
        twv = [[A.bf16(NB) for _ in range(2)] for _ in range(2)]; xa1v = [[A.bf16(NB) for _ in range(2)] for _ in range(2)]
        sggv = [A.bf16(NB) for _ in range(2)]
        B_twv = [[Buf("tw%d_%d" % (p, d)) for d in range(2)] for p in range(2)]
        B_xa1v = [[Buf("xa%d_%d" % (p, d)) for d in range(2)] for p in range(2)]
        B_sggv = [Buf("sgg0"), Buf("sgg1")]
        zbuf = A.bf16(8 * NB); B_z = [Buf("z%d" % h) for h in range(NHP)]
        xin = [A.f32(NB) for _ in range(2)]; B_xin = [Buf("xin0"), Buf("xin1")]
        xmid = [A.f32(NB) for _ in range(2)]; B_xmid = [Buf("xmid0"), Buf("xmid1")]

        class S_:
            pass

        def make_stream(si):
            S = S_()
            S.si = si
            S.b = [4 * si + i for i in range(4)]
            S.wkv = A.bf16(8 * 3 * 128); S.B_wkv = Buf("wkv%d" % si)
            S.wkvv = v4(S.wkv, 3, 128)
            S.ebuf = [A.f32(4 * 65) for _ in range(2)]; S.B_eb = [Buf("eb%d_0" % si), Buf("eb%d_1" % si)]
            for d in range(2):
                memset("pool", S.ebuf[d], 1.0, [S.B_eb[d]])
            names = ["r_f", "k_f", "kkraw", "v_f", "lns", "kk", "sgw", "asig", "lw", "lw2", "eneg", "t1", "kd",
                     "y0s", "b0s"]
            S.tf = {n: A.f32(NB) for n in names}
            S.Bt = {n: Buf("%s_%d" % (n, si)) for n in names}
            for al, src in (("ba", "t1"), ("y_f", "sgw"), ("yc", "asig"), ("yn", "lw"), ("tot", "lw2"), ("bon", "eneg")):
                S.tf[al] = S.tf[src]; S.Bt[al] = S.Bt[src]
            nbf = ["sq", "vT", "ktT", "btT", "rkd", "ybf", "sqb"]
            S.tb = {n: A.bf16(NB) for n in nbf}
            for n in nbf:
                S.Bt[n] = Buf("%s_%d" % (n, si))
            S.arT = A.bf16(2 * 2 * 128); S.Bt["arT"] = Buf("arT%d" % si)
            S.arv = v4(S.arT, 2, 128)
            S.tmb = A.bf16(8 * 128); S.Bt["tmb"] = Buf("tmb%d" % si)
            S.tmv = v4(S.tmb, 4, 128)
            S.Bt["Vtm"] = Buf("Vtm%d" % si)
            S.Vtv = None
            S.GB = A.bf16(256); S.Bt["GB"] = Buf("GB%d" % si); S.GBv = v3(S.GB, 128)
            S.Pb = [[A.bf16(384) for _ in range(2)] for _ in range(2)]
            S.BPb = [[Buf("Pb%d_%d_%d" % (si, e, i)) for i in range(2)] for e in range(2)]
            S.Qb = [[None, A.bf16(128)] for _ in range(2)]
            S.BQb = [[Buf("Qb%d_%d_%d" % (si, e, i)) for i in range(2)] for e in range(2)]
            S.BQ0 = [Buf("Q0_%d_%d" % (si, e)) for e in range(2)]
            S.XV = A.bf16(128); S.Bt["XV"] = Buf("XV%d" % si)
            S.WmT = [A.bf16(128) for _ in range(2)]; S.BWm = [Buf("WmT%d_0" % si), Buf("WmT%d_1" % si)]
            S.U0 = [A.f32(128) for _ in range(2)]; S.BU0 = [Buf("U0%d_0" % si), Buf("U0%d_1" % si)]
            S.Ub = [A.bf16(128) for _ in range(2)]; S.BUb = [Buf("Ub%d_0" % si), Buf("Ub%d_1" % si)]
            S.GAs = [A.bf16(1024) for _ in range(2)]; S.BGAs = [Buf("GAs%d_0" % si), Buf("GAs%d_1" % si)]
            S.htmp = A.f32(64); S.Bt["htmp"] = Buf("htmp%d" % si)
            return S
        ST = [make_stream(0), make_stream(1)]

        def load_wkv(S, hp):
            P.dma("pool", S.wkv, d_wkvp[hp], writes=[S.B_wkv], sembuf=S.B_wkv)
        load_wkv(ST[0], 0)
        load_wkv(ST[1], 1)
        P.barrier()

        def half(b, h):
            return PS[b][:, h * 256:(h + 1) * 256]

        def head_rows(e):
            return slice(64 * e, 64 * e + 64)

        mixidx = {"r": 0, "w": 1, "k": 2, "v": 3, "a": 4, "g": 5}
        wo_cnt = [0]
        proj_done = [0]
        STREAM_LAG = 0

        def hp_gen(S, hp, dirs, final, need_y, wb0, vp):
            tf, tb, Bt = S.tf, S.tb, S.Bt
            tw, xa1, sgg = twv[vp], xa1v[vp], sggv[vp]
            B_tw, B_xa1, B_sgg = B_twv[vp], B_xa1v[vp], B_sggv[vp]
            arv, tmv, Vtv = S.arv, S.tmv, S.Vtv
            b0, b1, b2, b3 = S.b
            PB = PSB
            hc0 = hp * 128
            first_dir = dirs[0]
            ps_k = half(b0, 0); ps_v = half(b2, 0); ps_r = half(b1, 0)
            for c in range(8):
                mm(ps_k, S.wkvv[:, c, 1, :], v3(xs3["k"], NB)[:, c, :], c == 0, c == 7, [S.B_wkv, B_xs["k"]], [PB[b0]])
            for c in range(8):
                mm(ps_v, S.wkvv[:, c, 2, :], v3(xs3["v"], NB)[:, c, :], c == 0, c == 7, [S.B_wkv, B_xs["v"]], [PB[b2]])
            if need_y:
                for c in range(8):
                    mm(ps_r, S.wkvv[:, c, 0, :], v3(xs3["r"], NB)[:, c, :], c == 0, c == 7, [S.B_wkv, B_xs["r"]], [PB[b1]])
            if final and need_y and len(dirs) == 1:
                P.dma("sp", tf["y0s"], s_y0[0, hp, :, wb0:wb0 + NB], reads=[B_sy0], writes=[Bt["y0s"]], sembuf=Bt["y0s"])
                P.dma("sp", tf["b0s"], s_y0[1, hp, :, wb0:wb0 + NB], reads=[B_sy0], writes=[Bt["b0s"]], sembuf=Bt["b0s"])
            load_wkv(S, (hp + 2) % NHP)
            proj_done[0] += 1
            yield
            if need_y:
                cp("act", tf["r_f"], ps_r, [PB[b1]], [Bt["r_f"]])
            act(tf["kkraw"], ps_k, AF.Copy, [PB[b0], B_vec], [Bt["kkraw"]], scale=V("k_k", hp))
            act(tb["sq"], ps_k, AF.Square, [PB[b0], B_vec], [Bt["sq"]], scale=V("k_k", hp))
            cp("dve", tf["v_f"], ps_v, [PB[b2]], [Bt["v_f"]])
            cp("dve", tf["k_f"], ps_k, [PB[b0]], [Bt["k_f"]])
            cp("act", tb["vT"], ps_v, [PB[b2]], [Bt["vT"]])
            yield
            ps_ss = half(b1, 1)
            mm(ps_ss, CB("blk1"), tb["sq"], True, True, [B_cstb, Bt["sq"]], [PB[b1]])
            yield
            act(tf["lns"], ps_ss, AF.Ln, [PB[b1]], [Bt["lns"]], bias=1e-30)
            act(tf["lns"], tf["lns"], AF.Exp, [Bt["lns"]], [Bt["lns"]], scale=-0.5)
            yield
            tt("dve", tf["kk"], tf["kkraw"], tf["lns"], ALU.mult, [Bt["kkraw"], Bt["lns"]], [Bt["kk"]])
            yield
            for d in dirs:
                ebuf = S.ebuf; B_eb = S.B_eb
                ps_w = half(b0, 0); ps_a = half(b0, 1)
                mm(ps_w, w2[d][0:64, hc0:hc0 + 128], tw[d][0:64, :], True, True, [B_w, B_tw[d]], [PB[b0]])
                mm(ps_a, a2[d][0:64, hc0:hc0 + 128], xa1[d][0:64, :], True, True, [B_w, B_xa1[d]], [PB[b0]])
                yield
                act(tf["sgw"], ps_w, AF.Sigmoid, [PB[b0], B_vec], [Bt["sgw"]], bias=V("w0", d * 8 + hp))
                act(tf["asig"], ps_a, AF.Sigmoid, [PB[b0], B_vec], [Bt["asig"]], bias=V("a0", d * 8 + hp))
                yield
                P.op("dve", lambda e: e.tensor_tensor_scan(tf["lw"], C("maskc"), tf["sgw"], 0.0, ALU.mult, ALU.add),
                     [B_cst, Bt["sgw"]], [Bt["lw"]])
                ts("dve", tf["t1"], tf["asig"], -1.0, ALU.add, [Bt["asig"], B_vec], [Bt["t1"]], s2=V("k_a", hp), op1=ALU.mult)
                stt(tf["kd"], tf["t1"], 1.0, tf["k_f"], ALU.add, ALU.mult, [Bt["t1"], Bt["k_f"]], [Bt["kd"]])
                yield
                if d == 0:
                    lwx = tf["lw"]; Blwx = Bt["lw"]
                else:
                    tt("pool", tf["t1"], tf["sgw"], tf["lw"], ALU.subtract, [Bt["sgw"], Bt["lw"]], [Bt["t1"]])
                    yield
                    totb = v3(tf["lw"], 64)[:, :, 63:64].broadcast_to([128, 4, 64])
                    tt("dve", v3(tf["lw2"], 64), v3(tf["t1"], 64), totb, ALU.add, [Bt["t1"], Bt["lw"]], [Bt["lw2"]])
                    yield
                    lwx = tf["lw2"]; Blwx = Bt["lw2"]
                ebv = v3(ebuf[d], 65)
                if d == 0:
                    epos = ebv[:, :, 1:65]; eprev = ebv[:, :, 0:64]; cwcol = 64
                else:
                    epos = ebv[:, :, 0:64]; eprev = ebv[:, :, 1:65]; cwcol = 0
                act(epos, v3(lwx, 64), AF.Exp, [Blwx], [B_eb[d]], scale=-CDEC)
                act(tf["eneg"], lwx, AF.Exp, [Blwx], [Bt["eneg"]], scale=CDEC)
                tt("dve", tf["ba"], tf["kk"], tf["asig"], ALU.mult, [Bt["kk"], Bt["asig"]], [Bt["ba"]])
                yield
                tt("pool", tb["ktT"], tf["kd"], tf["eneg"], ALU.mult, [Bt["kd"], Bt["eneg"]], [Bt["ktT"]])
                a_out = arv[:, :, 0, :].rearrange("p t (c j) -> p t c j", j=64)
                r_out = arv[:, :, 1, :].rearrange("p t (c j) -> p t c j", j=64)
                kk4 = tf["kk"].rearrange("p (t c j) -> p t c j", c=2, j=64)
                ep4 = eprev.rearrange("p (t c) j -> p t c j", c=2)
                eo4 = epos.rearrange("p (t c) j -> p t c j", c=2)
                stt(a_out, kk4, -1.0, ep4, ALU.mult, ALU.mult, [Bt["kk"], B_eb[d]], [Bt["arT"]])
                yield
                tt("pool", tb["btT"], tf["ba"], tf["eneg"], ALU.mult, [Bt["ba"], Bt["eneg"]], [Bt["btT"]])
                if need_y:
                    r4 = tf["r_f"].rearrange("p (t c j) -> p t c j", c=2, j=64)
                    tt("dve", r_out, r4, eo4, ALU.mult, [Bt["r_f"], B_eb[d]], [Bt["arT"]])
                    stt(tb["rkd"], tf["kd"], V("r_k", hp), tf["r_f"], ALU.mult, ALU.mult, [Bt["kd"], Bt["r_f"], B_vec], [Bt["rkd"]])
                yield
                pst = PS[b3].bitcast(BF16)
                for tl in range(2):
                    srcs = [tb["btT"][:, tl * 128:(tl + 1) * 128], tb["ktT"][:, tl * 128:(tl + 1) * 128], arv[:, tl, 0, :]]
                    rds = [[Bt["btT"]], [Bt["ktT"]], [Bt["arT"]]]
                    for ki in range(3):
                        tr(pst[:, (tl * 4 + ki) * 128:(tl * 4 + ki + 1) * 128], srcs[ki], CB("ident"), rds[ki] + [B_cstb], [PB[b3]])
                    if d == first_dir:
                        tr(pst[:, (tl * 4 + 3) * 128:(tl * 4 + 4) * 128], tb["vT"][:, tl * 128:(tl + 1) * 128], CB("ident"),
                           [Bt["vT"], B_cstb], [PB[b3]])
                yield
                pst4 = v4(pst, 4, 128)
                if d == first_dir:
                    cp("act", tmv[:, :, :, :], pst4[:, :, :, :], [PB[b3]], [Bt["tmb"], Bt["Vtm"]])
                else:
                    cp("act", tmv[:, :, 0:3, :], pst4[:, :, 0:3, :], [PB[b3]], [Bt["tmb"]])
                yield
                Vt = lambda tl, e: tmv[:, tl, 3, e * 64:(e + 1) * 64]
                tiles = [0, 1] if d == 0 else [1, 0]
                for tl in tiles:
                    gasv = v3(S.GAs[tl], 512)
                    for e in range(2):
                        R = head_rows(e)
                        bA = b1 if e == 0 else b3
                        psA = PS[bA]
                        bT = tb["btT"][R, tl * 128:(tl + 1) * 128]
                        kT = tb["ktT"][R, tl * 128:(tl + 1) * 128]
                        ar = arv[R, tl, :, :]
                        aT = arv[R, tl, 0, :]
                        psBe = PS[b0][:, 384:512]
                        mm(psA[:, 0:256], bT, ar, True, True, [Bt["btT"], Bt["arT"]], [PB[bA]])
                        mm(psA[:, 256:512], kT, ar, True, True, [Bt["ktT"], Bt["arT"]], [PB[bA]])
                        mm(psBe, aT, bT, True, True, [Bt["btT"], Bt["arT"]], [PB[b0]])
                        yield
                        tt("dve", gasv[:, e, :], psA[:, :], mt2[d][:, e * 512:(e + 1) * 512], ALU.mult, [PB[bA], B_cstb], [S.BGAs[tl]])
                        tt("dve", S.GBv[:, e, :], psBe, mb2[d][:, e * 128:(e + 1) * 128], ALU.mult, [PB[b0], B_cstb], [Bt["GB"]])
                        yield
                        tt("pool", S.Pb[e][1][:, 256:384], gasv[:, e, 0:128], id2[:, 0:128], ALU.add, [S.BGAs[tl], B_cstb], [S.BQ0[e]])
                    psx = PS[b0][:, 0:128]
                    for e in range(2):
                        mm(psx[:, e * 64:(e + 1) * 64], gasv[:, e, 256:384], Vt(tl, e), True, True, [S.BGAs[tl], Bt["tmb"]], [PB[b0]])
                    yield
                    cp("act", S.XV, psx, [PB[b0]], [Bt["XV"]])
                    for lev in range(1, 6):
                        pbi = lev % 2
                        for e in range(2):
                            bk = b2 + e
                            if lev == 1:
                                Pm = S.GBv[:, e, :]; PTm = gasv[:, e, 0:128]; rr = [Bt["GB"], S.BGAs[tl]]
                                mm(PS[bk][:, 0:128], PTm, Pm, True, True, rr, [PB[bk]])
                                mm(PS[bk][:, 128:256], Pm, PTm, True, True, rr, [PB[bk]])
                            else:
                                src = S.Pb[e][1 - pbi]
                                rr = [S.BPb[e][1 - pbi]] + ([S.BQ0[e]] if lev == 2 else [])
                                mm(PS[bk][:, 0:128], src[:, 128:256], src[:, 0:128], True, True, rr, [PB[bk]])
                                mm(PS[bk][:, 128:384], src[:, 0:128], src[:, 128:384], True, False, rr, [PB[bk]])
                                mm(PS[bk][:, 256:384], CB("ident"), src[:, 256:384], False, True, rr + [B_cstb], [PB[bk]])
                        yield
                        for e in range(2):
                            bk = b2 + e
                            n_ = 256 if lev == 1 else 384
                            cp("act" if e == 0 else "dve", S.Pb[e][pbi][:, 0:n_], PS[bk][:, 0:n_], [PB[bk]], [S.BPb[e][pbi]])
                        yield
                    for e in range(2):
                        bk = b2 + e
                        src = S.Pb[e][1]
                        mm(PS[bk][:, 256:384], src[:, 0:128], src[:, 256:384], True, False, [S.BPb[e][1]], [PB[bk]])
                        mm(PS[bk][:, 256:384], CB("ident"), src[:, 256:384], False, True, [S.BPb[e][1], B_cstb], [PB[bk]])
                    yield
                    for e in range(2):
                        bk = b2 + e
                        cp("act" if e == 0 else "dve", S.Qb[e][1], PS[bk][:, 256:384], [PB[bk]], [S.BQb[e][1]])
                    yield
                    TT_ = [S.Qb[0][1], S.Qb[1][1]]
                    BTT = [S.BQb[0][1], S.BQb[1][1]]
                    psw = PS[b0][:, 128:256]
                    for e in range(2):
                        mm(psw[head_rows(e), :], tmv[:, tl, 2, e * 64:(e + 1) * 64], TT_[e], True, True, [Bt["tmb"], BTT[e]], [PB[b0]])
                    psu0 = PS[b0][:, 256:384]
                    for e in range(2):
                        mm(psu0[:, e * 64:(e + 1) * 64], TT_[e], S.XV[:, e * 64:(e + 1) * 64], True, True, [BTT[e], Bt["XV"]], [PB[b0]])
                    yield
                    cp("act", S.WmT[tl], psw, [PB[b0]], [S.BWm[tl]])
                    cp("act", S.U0[tl], psu0, [PB[b0]], [S.BU0[tl]])
                    yield
                psY = PS[b1][:, 256:512]
                for tl in tiles:
                    gasv = v3(S.GAs[tl], 512)
                    chunks = [0, 1] if d == 0 else [1, 0]
                    for cq in chunks:
                        Sc = slice(64 * cq, 64 * cq + 64)
                        chl = tl * 2 + cq
                        cur = hb_par[d][hp]
                        nxt = 1 - cur
                        psU = PS[b3][:, 0:128]
                        for e in range(2):
                            R = head_rows(e)
                            mm(psU[Sc, e * 64:(e + 1) * 64], S.WmT[tl][R, cq * 64:(cq + 1) * 64], HB(cur, d, hp)[R, :], True, True,
                               [S.BWm[tl], B_Hb[cur][d][hp]], [PB[b3]])
                        yield
                        tt("dve", S.Ub[tl][Sc, :], psU[Sc, :], S.U0[tl][Sc, :], ALU.add, [PB[b3], S.BU0[tl]], [S.BUb[tl]])
                        yield
                        psH = PS[b3][:, 128:192]
                        for e in range(2):
                            R = head_rows(e)
                            mm(psH[R, :], tmv[Sc, tl, 0, e * 64:(e + 1) * 64], S.Ub[tl][Sc, e * 64:(e + 1) * 64], True, False,
                               [Bt["tmb"], S.BUb[tl]], [PB[b3]])
                            mm(psH[R, :], tmv[Sc, tl, 1, e * 64:(e + 1) * 64], tmv[Sc, tl, 3, e * 64:(e + 1) * 64], False, True,
                               [Bt["tmb"]], [PB[b3]])
                        if need_y:
                            yc0 = tl * 128 + cq * 64
                            for e in range(2):
                                R = head_rows(e)
                                mm(psY[R, yc0:yc0 + 64], HB(cur, d, hp)[R, :], arv[R, tl, 1, cq * 64:(cq + 1) * 64], True, False,
                                   [B_Hb[cur][d][hp], Bt["arT"]], [PB[b1]])
                                mm(psY[R, yc0:yc0 + 64], S.Ub[tl][Sc, e * 64:(e + 1) * 64],
                                   gasv[Sc, e, 128 + cq * 64:128 + cq * 64 + 64], False, False, [S.BUb[tl], S.BGAs[tl]], [PB[b1]])
                                mm(psY[R, yc0:yc0 + 64], tmv[Sc, tl, 3, e * 64:(e + 1) * 64],
                                   gasv[Sc, e, 384 + cq * 64:384 + cq * 64 + 64], False, True, [Bt["tmb"], S.BGAs[tl]], [PB[b1]])
                        yield
                        cw = v3(ebuf[d], 65)[:, chl, cwcol:cwcol + 1]
                        act(S.htmp, HF(d, hp), AF.Copy, [B_Hf[d][hp], B_eb[d]], [Bt["htmp"]], scale=cw)
                        yield
                        stt(HB(nxt, d, hp), psH, cw, S.htmp, ALU.mult, ALU.add, [PB[b3], B_eb[d], Bt["htmp"]], [B_Hb[nxt][d][hp]])
                        stt(HF(d, hp), psH, cw, S.htmp, ALU.mult, ALU.add, [PB[b3], B_eb[d], Bt["htmp"]], [B_Hf[d][hp]])
                        hb_par[d][hp] = nxt
                        yield
                if need_y:
                    ps_s = half(b0, 0)
                    mm(ps_s, CB("blk1"), tb["rkd"], True, True, [B_cstb, Bt["rkd"]], [PB[b0]])
                    is_first = (d == dirs[0]) and not (final and len(dirs) == 1)
                    if is_first:
                        cp("act", tf["y0s"], psY, [PB[b1]], [Bt["y0s"]])
                        yield
                        tt("dve", tf["b0s"], ps_s, tf["v_f"], ALU.mult, [PB[b0], Bt["v_f"]], [Bt["b0s"]])
                        if not final:
                            P.dma("sp", s_y0[0, hp, :, wb0:wb0 + NB], tf["y0s"], reads=[Bt["y0s"]], writes=[B_sy0], sembuf=Bt["y0s"])
                            P.dma("sp", s_y0[1, hp, :, wb0:wb0 + NB], tf["b0s"], reads=[Bt["b0s"]], writes=[B_sy0], sembuf=Bt["b0s"])
                        yield
                    else:
                        tt("dve", tf["y_f"], psY, tf["y0s"], ALU.add, [PB[b1], Bt["y0s"]], [Bt["y_f"]])
                        yield
                        cp("act", tb["ybf"], tf["y_f"], [Bt["y_f"]], [Bt["ybf"]])
                        tt("dve", tf["bon"], ps_s, tf["v_f"], ALU.mult, [PB[b0], Bt["v_f"]], [Bt["bon"]])
                        yield
                        ps_m = half(b1, 0)
                        mm(ps_m, CB("blk64"), tb["ybf"], True, True, [B_cstb, Bt["ybf"]], [PB[b1]])
                        yield
                        tt("dve", tf["yc"], tf["y_f"], ps_m, ALU.subtract, [Bt["y_f"], PB[b1]], [Bt["yc"]])
                        yield
                        act(tb["sqb"], tf["yc"], AF.Square, [Bt["yc"]], [Bt["sqb"]])
                        yield
                        ps_v2 = half(b1, 1)
                        mm(ps_v2, CB("blk64"), tb["sqb"], True, True, [B_cstb, Bt["sqb"]], [PB[b1]])
                        yield
                        act(tf["yn"], ps_v2, AF.Ln, [PB[b1]], [Bt["yn"]], bias=GN_EPS)
                        act(tf["yn"], tf["yn"], AF.Exp, [Bt["yn"]], [Bt["yn"]], scale=-0.5)
                        yield
                        tt("dve", tf["yn"], tf["yc"], tf["yn"], ALU.mult, [Bt["yc"], Bt["yn"]], [Bt["yn"]])
                        yield
                        ts("dve", tf["tot"], tf["yn"], V("lnx_w", hp), ALU.mult, [Bt["yn"], B_vec], [Bt["tot"]],
                           s2=V("lnx_b", hp), op1=ALU.add)
                        ps_g = half(b0, 1)
                        mm(ps_g, g2[:, hc0:hc0 + 128], sgg, True, True, [B_w, B_sgg], [PB[b0]])
                        yield
                        tt("dve", tf["tot"], tf["tot"], tf["b0s"], ALU.add, [Bt["tot"], Bt["b0s"]], [Bt["tot"]])
                        yield
                        tt("dve", tf["tot"], tf["tot"], tf["bon"], ALU.add, [Bt["tot"], Bt["bon"]], [Bt["tot"]])
                        yield
                        tt("dve", v3(zbuf, NB)[:, hp, :], tf["tot"], ps_g, ALU.mult, [Bt["tot"], PB[b0]], [B_z[hp]])
                        yield

        def vis_params(kind, si, blk):
            if kind == "p":
                return HB_P[si] + blk * NB, XB_P[si] + blk * NB, 0
            return HB_S + blk * NB, XB_S + blk * NB, 1

        def prologue_gen(vis, vp):
            kind, si, blk, dirs, final, need_y = vis
            hb0, xb0, cd = vis_params(kind, si, blk)
            tw, xa1, sgg = twv[vp], xa1v[vp], sggv[vp]
            B_tw, B_xa1, B_sgg = B_twv[vp], B_xa1v[vp], B_sggv[vp]
            hc = hv[:, :, hb0:hb0 + NB]; hp_ = hv[:, :, hb0 - 1:hb0 - 1 + NB]; hn = hv[:, :, hb0 + 1:hb0 + 1 + NB]
            tt("dve", v3(dd1, NB), hp_, hc, ALU.subtract, [B_h], [B_dd])
            tt("pool", v3(dd2, NB), hn, hc, ALU.subtract, [B_h], [B_dd])
            yield

            def make_xs(m, dst, Bdst):
                mi = mixidx[m]
                dv = v3(dst, NB)
                for c in range(8):
                    mu0 = V("mu", c * 12 + mi); mu1 = V("mu", c * 12 + 6 + mi)
                    tsl = tmpx[c % 2]
                    stt(tsl, v3(dd1, NB)[:, c, :], mu0, hc[:, c, :], ALU.mult, ALU.add, [B_dd, B_h, B_vec], [B_tmpx[c % 2]])
                    stt(dv[:, c, :], v3(dd2, NB)[:, c, :], mu1, tsl, ALU.mult, ALU.add, [B_dd, B_tmpx[c % 2], B_vec], [Bdst])
                    if c % 2 == 1:
                        yield
            need = ["k", "v"] + (["r"] if need_y else [])
            for m in need:
                yield from make_xs(m, xs3[m], B_xs[m])
            yield from make_xs("w", xst, B_xs["t"])
            pro_ps = [(PS[2][:, 384:512], PSB[2]), (PS[6][:, 384:512], PSB[6])]
            for d in dirs:
                for hf in range(2):
                    ps_, Bp_ = pro_ps[hf]
                    for c in range(8):
                        mm(ps_[0:64, :], v3(w1[d], 64)[:, c, :], v3(xst, NB)[:, c, hf * 128:(hf + 1) * 128], c == 0, c == 7,
                           [B_w, B_xs["t"]], [Bp_])
                    act(tw[d][0:64, hf * 128:(hf + 1) * 128], ps_[0:64, :], AF.Tanh, [Bp_], [B_tw[d]])
                    yield
            yield from make_xs("a", xst, B_xs["t"])
            for d in dirs:
                for hf in range(2):
                    ps_, Bp_ = pro_ps[hf]
                    for c in range(8):
                        mm(ps_[0:64, :], v3(a1[d], 64)[:, c, :], v3(xst, NB)[:, c, hf * 128:(hf + 1) * 128], c == 0, c == 7,
                           [B_w, B_xs["t"]], [Bp_])
                    cp("act", xa1[d][0:64, hf * 128:(hf + 1) * 128], ps_[0:64, :], [Bp_], [B_xa1[d]])
                    yield
            if final:
                yield from make_xs("g", xst, B_xs["t"])
                for hf in range(2):
                    ps_, Bp_ = pro_ps[hf]
                    for c in range(8):
                        mm(ps_, v3(g1, 128)[:, c, :], v3(xst, NB)[:, c, hf * 128:(hf + 1) * 128], c == 0, c == 7,
                           [B_w, B_xs["t"]], [Bp_])
                    act(sgg[:, hf * 128:(hf + 1) * 128], ps_, AF.Sigmoid, [Bp_], [B_sgg])
                    yield

        def run_gens(gens):
            alive = [True] * len(gens)
            while any(alive):
                for gi in range(len(gens)):
                    if alive[gi]:
                        try:
                            next(gens[gi])
                        except StopIteration:
                            alive[gi] = False

        def visit_main(vis, vp, nxt_vis):
            kind, si, blk, dirs, final, need_y = vis
            hb0, xb0, cd = vis_params(kind, si, blk)
            wb0 = blk * NB
            def stream_chain(S, hps, lag):
                for _ in range(lag):
                    yield
                for hp in hps:
                    yield from hp_gen(S, hp, dirs, final, need_y, wb0, vp)

            def late_prologue():
                while proj_done[0] < NHP:
                    yield
                if nxt_vis is not None:
                    yield from prologue_gen(nxt_vis, 1 - vp)
            proj_done[0] = 0
            run_gens([stream_chain(ST[0], [0, 2, 4, 6], 0), stream_chain(ST[1], [1, 3, 5, 7], STREAM_LAG), late_prologue()])
            if final:
                for dc in range(8):
                    sl = wo_cnt[0] % 2
                    wo_cnt[0] += 1
                    if wo_cnt[0] == 1:
                        P.dma("pool", wob[0], d_wop[0], writes=[B_wob[0]], sembuf=B_wob[0])
                    P.dma("pool", wob[1 - sl], d_wop[(dc + 1) % 8], writes=[B_wob[1 - sl]], sembuf=B_wob[1 - sl])
                    P.dma("sp", xin[sl], s_x[:, dc, xb0:xb0 + NB], reads=[B_sx], writes=[B_xin[sl]], sembuf=B_xin[sl])
                    pso = PS[dc % 2][:, 256:512]
                    for hp in range(NHP):
                        mm(pso, v3(wob[sl], 128)[:, hp, :], v3(zbuf, NB)[:, hp, :], hp == 0, hp == NHP - 1, [B_wob[sl], B_z[hp]], [PSB[dc % 2]])
                    stt(xmid[sl], pso, MOD(0, 2, dc, cd), xin[sl], ALU.mult, ALU.add,
                        [PSB[dc % 2], B_mod[0], B_xin[sl]], [B_xmid[sl]])
                    P.dma("sp", s_x[:, dc, xb0:xb0 + NB], xmid[sl], reads=[B_xmid[sl]], writes=[B_sx], sembuf=B_xmid[sl])

        def init_state(kind):
            for d in range(2):
                for hp in range(NHP):
                    if kind == "p":
                        memset("pool", HF(d, hp), 0.0, [B_Hf[d][hp]])
                    else:
                        cp("pool", HF(d, hp), HS0(d, hp), [B_Hs0], [B_Hf[d][hp]])
                    cur = hb_par[d][hp]
                    cp("pool", HB(cur, d, hp), HF(d, hp), [B_Hf[d][hp]], [B_Hb[cur][d][hp]])

        B_ns = Buf("ns")
        visits = []
        for si in range(2):
            visits.append(("p", si, 0, [0, 1], True, True))
        for blk in range(WIN // NB):
            visits.append(("s", 0, blk, [0], False, True))
        for blk in range(LS // NB - 1, WIN // NB - 1, -1):
            visits.append(("s", 0, blk, [1], False, False))
        for blk in range(WIN // NB - 1, -1, -1):
            visits.append(("s", 0, blk, [1], True, True))
        run_gens([prologue_gen(visits[0], 0)])
        for i, vis in enumerate(visits):
            vp = i % 2
            if vis[0] == "p":
                init_state("p")
            elif i == 2:
                init_state("s")
            visit_main(vis, vp, visits[i + 1] if i + 1 < len(visits) else None)
            if vis[0] == "p":
                for d in range(2):
                    for hp in range(NHP):
                        P.dma("sp", o_ns[vis[1], d, hp], HF(d, hp), reads=[B_Hf[d][hp]], writes=[B_ns], bulk="out")
        A.release(m0)


    def load_x(xall, B_x):
        P.dma("sp", v3(xall, TT), s_x, reads=[B_sx], writes=[B_x], bulk="lx")

    def ffn_layer(layer, n_s_up, n_s_dn):
        m0 = A.mark()
        NTK = 2 * LP + n_s_up
        h2 = A.bf16(8 * NTK); B_h2 = Buf("h2")
        h2v = v3(h2, NTK)
        actb = A.bf16(NFC * NTK); B_act = Buf("act")
        actv = v3(actb, NTK)
        m1 = A.mark()
        xg2 = [A.f32(8 * 512) for _ in range(2)]; Bxg2 = [Buf("xg_f0"), Buf("xg_f1")]
        nt = norm_tmps(512)
        pieces = []
        col = 0
        while col < NTK:
            n = min(512, NTK - col)
            pieces.append((col, n))
            col += n

        def ld(i):
            c_, n_ = pieces[i]
            P.dma("sp", v3(xg2[i % 2], 512)[:, :, 0:n_], s_x[:, :, c_:c_ + n_], reads=[B_sx], writes=[Bxg2[i % 2]], sembuf=Bxg2[i % 2])
        ld(0)
        for i, (col, n) in enumerate(pieces):
            if i + 1 < len(pieces):
                ld(i + 1)
            cd = 0 if col < 2 * LP else 1
            rmsnorm_fm(v3(xg2[i % 2], 512)[:, :, 0:n], n, Bxg2[i % 2], lambda c, cd=cd: GM(layer, 1, c, cd),
                       lambda c, cd=cd: MOD(layer, 3, c, cd), lambda c, col=col, n=n: h2v[:, c, col:col + n], B_h2, nt)
        A.release(m1)
        rows = n_s_up // 64
        UPW = (rows + 2) * 66
        upad_s = [A.bf16(UPW) for _ in range(2)]; upad_p = [A.bf16(2 * 258) for _ in range(2)]
        B_up_s = [Buf("ups0"), Buf("ups1")]; B_up_p = [Buf("upp0"), Buf("upp1")]
        for i in range(2):
            memset("pool", upad_s[i], 0.0, [B_up_s[i]])
            memset("pool", upad_p[i], 0.0, [B_up_p[i]])
        wup = [A.bf16(8 * 256) for _ in range(2)]; B_wup = [Buf("wup0"), Buf("wup1")]
        dg = [A.bf16(2 * 9 * 128) for _ in range(2)]; B_dg = [Buf("dg0"), Buf("dg1")]
        vv = [A.f32(512) for _ in range(2)]; B_vv = [Buf("vv0"), Buf("vv1")]
        sg = [A.f32(512) for _ in range(2)]; B_sg = [Buf("sg0"), Buf("sg1")]
        rows_c = n_s_dn // 64
        rows_u = min(rows, rows_c + 1)

        def mk_tiles(nrows):
            t_, r0_ = [], 0
            while r0_ < nrows:
                nr_ = min(8, nrows - r0_)
                t_.append((r0_, nr_))
                r0_ += nr_
            return t_
        s_tiles = mk_tiles(rows_u)
        c_tiles = mk_tiles(rows_c)
        it = 0
        P.dma("pool", wup[0], d_upp[layer, 0], writes=[B_wup[0]], sembuf=B_wup[0])
        for j in range(NFC):
            sl = j % 2
            wv_ = v4(wup[sl], 2, 128)
            if j + 1 < NFC:
                P.dma("pool", wup[1 - sl], d_upp[layer, j + 1], writes=[B_wup[1 - sl]], sembuf=B_wup[1 - sl])
            dgv = v4(dg[sl], 9, 128)
            for vg in range(2):
                fc = j + vg * NFC
                idb = C("ident").unsqueeze(1).broadcast_to([128, 9, 128])
                wcb = V("conv_w", (layer * 44 + fc) * 9, 9).unsqueeze(2).broadcast_to([128, 9, 128])
                tt("dve", dgv[:, vg, :, :], idb, wcb, ALU.mult, [B_cst, B_vec], [B_dg[sl]])
            for vg in range(2):
                ps = PS[vg]
                for c in range(8):
                    mm(ps[:, 0:512], wv_[:, c, vg, :], h2v[:, c, 0:512], c == 0, c == 7, [B_wup[sl], B_h2], [PSB[vg]])
                upv = v3(upad_p[vg], 258)[:, :, 1:257]
                cp("act" if vg == 0 else "dve", upv, v3(ps[:, 0:512], 256), [PSB[vg]], [B_up_p[vg]])
            for vg in range(2):
                ps = PS[2 + vg]
                fc = j + vg * NFC
                for ti_, tp in enumerate((3, 4, 5)):
                    dj = tp - 4
                    rhs = v3(upad_p[vg], 258)[:, :, 1 + dj:257 + dj]
                    mm(v3(ps[:, 0:512], 256), dgv[:, vg, tp, :], rhs, ti_ == 0, ti_ == 2, [B_dg[sl], B_up_p[vg]], [PSB[2 + vg]])
                bia = V("conv_b", layer * 44 + fc)
                k_ = it % 2
                if vg == 0:
                    act(vv[k_], ps[:, 0:512], AF.Identity, [PSB[2], B_vec], [B_vv[k_]], bias=bia)
                else:
                    act(sg[k_], ps[:, 0:512], AF.Silu, [PSB[3], B_vec], [B_sg[k_]], bias=bia)
            tt("dve", actv[:, j, 0:512], vv[it % 2], sg[it % 2], ALU.mult, [B_vv[it % 2], B_sg[it % 2]], [B_act])
            it += 1
            for vg in range(2):
                for (r0, nr) in s_tiles:
                    n = nr * 64
                    pb = 4 + (r0 // 8) % 2
                    ps = PS[pb]
                    t0 = 2 * LP + r0 * 64
                    for c in range(8):
                        mm(ps[:, 0:n], wv_[:, c, vg, :], h2v[:, c, t0:t0 + n], c == 0, c == 7, [B_wup[sl], B_h2], [PSB[pb]])
                    upv = v3(upad_s[vg], 66)[:, 1 + r0:1 + r0 + nr, 1:65]
                    cp("act" if (r0 // 8) % 2 == 0 else "dve", upv, v3(ps[:, 0:n], 64), [PSB[pb]], [B_up_s[vg]])
            for (r0, nr) in c_tiles:
                n = nr * 64
                t0 = 2 * LP + r0 * 64
                for vg in range(2):
                    ps = PS[6 + vg]
                    fc = j + vg * NFC
                    for tp in range(9):
                        di, dj = tp // 3 - 1, tp % 3 - 1
                        rhs = v3(upad_s[vg], 66)[:, 1 + r0 + di:1 + r0 + di + nr, 1 + dj:65 + dj]
                        mm(v3(ps[:, 0:n], 64), dgv[:, vg, tp, :], rhs, tp == 0, tp == 8, [B_dg[sl], B_up_s[vg]], [PSB[6 + vg]])
                    bia = V("conv_b", layer * 44 + fc)
                    k_ = it % 2
                    if vg == 0:
                        act(vv[k_][:, 0:n], ps[:, 0:n], AF.Identity, [PSB[6], B_vec], [B_vv[k_]], bias=bia)
                    else:
                        act(sg[k_][:, 0:n], ps[:, 0:n], AF.Silu, [PSB[7], B_vec], [B_sg[k_]], bias=bia)
                tt("dve", actv[:, j, t0:t0 + n], vv[it % 2][:, 0:n], sg[it % 2][:, 0:n], ALU.mult,
                   [B_vv[it % 2], B_sg[it % 2]], [B_act])
                it += 1
        A.release(m1)
        NDN = 2 * LP + n_s_dn
        wdn = [A.bf16(NFC * 128) for _ in range(2)]; B_wdn = [Buf("wdn0"), Buf("wdn1")]
        xall = A.f32(8 * NDN); B_xall = Buf("xall")
        xav = v3(xall, NDN)
        P.dma("sp", xav, s_x[:, :, 0:NDN], reads=[B_sx], writes=[B_xall], sembuf=B_xall)
        k = 0
        P.dma("pool", wdn[0], d_dnp[layer, 0], writes=[B_wdn[0]], sembuf=B_wdn[0])
        for dc in range(8):
            sl = dc % 2
            if dc + 1 < 8:
                P.dma("pool", wdn[1 - sl], d_dnp[layer, dc + 1], writes=[B_wdn[1 - sl]], sembuf=B_wdn[1 - sl])
            col = 0
            while col < NDN:
                n = min(512, NDN - col)
                cd = 0 if col < 2 * LP else 1
                pb = k % 4
                k += 1
                for fc in range(NFC):
                    mm(PS[pb][:, 0:n], v3(wdn[sl], 128)[:, fc, :], actv[:, fc, col:col + n], fc == 0, fc == NFC - 1,
                       [B_wdn[sl], B_act], [PSB[pb]])
                stt(xav[:, dc, col:col + n], PS[pb][:, 0:n], MOD(layer, 5, dc, cd), xav[:, dc, col:col + n], ALU.mult, ALU.add,
                    [PSB[pb], B_mod[layer], B_xall], [B_xall])
                col += n
        P.dma("sp", s_x[:, :, 0:NDN], xav, reads=[B_xall], writes=[B_sx], sembuf=B_xall)
        A.release(m0)
        P.barrier()


    def sgu_layer(layer, n_s):
        m0 = A.mark()
        NTK = 2 * LP + n_s
        win = A.bf16(8 * 2 * E2); B_win = Buf("sg_in")
        winv = v3(win, 2 * E2)
        for c in range(8):
            P.dma("pool", winv[:, c, :], d_sgin[c * 128:(c + 1) * 128, :], writes=[B_win], bulk="sgw")
        lnw = A.f32(E2); lnb = A.f32(E2); bs = A.f32(8); wsT = A.bf16(8 * 128); B_sgc = Buf("sgc")
        P.dma("sp", lnw, d_lnw, writes=[B_sgc], bulk="sgw")
        P.dma("sp", lnb, d_lnb, writes=[B_sgc], bulk="sgw")
        P.dma("sp", bs, d_bs, writes=[B_sgc], bulk="sgw")
        P.dma("pool", wsT, d_wsT, writes=[B_sgc], bulk="sgw")
        xgs = A.f32(8 * 512); B_xall = Buf("xg_s")
        xav_full = v3(xgs, 512)
        P.barrier()
        hg = A.bf16(8 * 512); B_hg = Buf("hg"); hgv = v3(hg, 512)
        nt = norm_tmps(512)
        class R_:
            pass

        def make_res(si):
            R = R_()
            R.u_f = A.bf16(E2); R.B_u = Buf("u_f%d" % si)
            R.v_f = A.f32(E2); R.B_v = Buf("v_f%d" % si)
            R.vln = A.bf16(E2); R.B_vln = Buf("vln%d" % si)
            R.gated = A.bf16(E2); R.B_gt = Buf("gated%d" % si)
            R.stats = A.f32(4 * 6); R.mv = A.f32(2); R.rs = A.f32(2); R.B_st = Buf("lnstats%d" % si)
            R.b = [4 * si + i for i in range(4)]
            R.si = si
            return R
        RS = [make_res(0), make_res(1)]
        gT = A.bf16(16 * 512); B_gT = [Buf("gT%d" % i) for i in range(4)]; gTv = v3(gT, 512)
        wso = [A.bf16(16 * 128) for _ in range(2)]; B_wso = [Buf("wso0"), Buf("wso1")]

        def chunk_gen(R, ck):
            u_f, v_f, vln, gated, stats, mv, rs = R.u_f, R.v_f, R.vln, R.gated, R.stats, R.mv, R.rs
            B_u, B_v, B_vln, B_gt, B_st = R.B_u, R.B_v, R.B_vln, R.B_gt, R.B_st
            bz = [R.b[0], R.b[1]]; bsp = R.b[2]; btr = R.b[3]
            for cb in range(8):
                pb = bz[cb % 2]
                for c in range(8):
                    mm(PS[pb][:, :], hgv[:, c, ck * 128:(ck + 1) * 128], winv[:, c, cb * 512:(cb + 1) * 512], c == 0, c == 7,
                       [B_hg, B_win], [PSB[pb]])
                yield
                if cb < 4:
                    act(u_f[:, cb * 512:(cb + 1) * 512], PS[pb][:, :], AF.Gelu, [PSB[pb]], [B_u])
                else:
                    act(v_f[:, (cb - 4) * 512:(cb - 3) * 512], PS[pb][:, :], AF.Gelu, [PSB[pb]], [B_v])
            yield
            for q in range(4):
                P.op("dve", lambda e, q=q: e.bn_stats(stats[:, q * 6:(q + 1) * 6], v_f[:, q * 512:(q + 1) * 512]), [B_v], [B_st])
            P.op("dve", lambda e: e.bn_aggr(mv, stats), [B_st], [B_st])
            yield
            act(rs[:, 0:1], mv[:, 1:2], AF.Ln, [B_st], [B_st], bias=LN_EPS)
            act(rs[:, 0:1], rs[:, 0:1], AF.Exp, [B_st], [B_st], scale=-0.5)
            yield
            stt(rs[:, 1:2], mv[:, 0:1], -1.0, rs[:, 0:1], ALU.mult, ALU.mult, [B_st], [B_st])
            ts("dve", v_f, v_f, rs[:, 0:1], ALU.mult, [B_v, B_st], [B_v], s2=rs[:, 1:2], op1=ALU.add)
            yield
            tt("dve", v_f, v_f, lnw, ALU.mult, [B_v, B_sgc], [B_v])
            yield
            tt("pool" if R.si == 0 else "dve", vln, v_f, lnb, ALU.add, [B_v, B_sgc], [B_vln])
            yield
            for g2_ in range(4):
                for gi in range(2):
                    g = g2_ * 2 + gi
                    mm(PS[bsp][:, gi * 256:(gi + 1) * 256], v3(wsT, 128)[:, g, :], vln[:, g * 256:(g + 1) * 256], True, True,
                       [B_sgc, B_vln], [PSB[bsp]])
                yield
                for gi in range(2):
                    g = g2_ * 2 + gi
                    stt(gated[:, g * 256:(g + 1) * 256], PS[bsp][:, gi * 256:(gi + 1) * 256], bs[:, g:g + 1],
                        u_f[:, g * 256:(g + 1) * 256], ALU.add, ALU.mult, [PSB[bsp], B_sgc, B_u], [B_gt])
                yield
            for h8 in range(2):
                pst = PS[btr].bitcast(BF16)
                for c8 in range(8):
                    cc = h8 * 8 + c8
                    tr(pst[:, c8 * 128:(c8 + 1) * 128], gated[:, cc * 128:(cc + 1) * 128], CB("ident"), [B_gt, B_cstb], [PSB[btr]])
                yield
                cp("act", gTv[:, h8 * 8:(h8 + 1) * 8, ck * 128:(ck + 1) * 128], v3(pst, 128), [PSB[btr]], [B_gT[ck]])
                yield

        def run_gens2(gens):
            alive = [True] * len(gens)
            while any(alive):
                for gi in range(len(gens)):
                    if alive[gi]:
                        try:
                            next(gens[gi])
                        except StopIteration:
                            alive[gi] = False
        col = 0
        kk_ = 0
        while col < NTK:
            n = min(512, NTK - col)
            cd = 0 if col < 2 * LP else 1
            xav = xav_full[:, :, 0:n]
            P.dma("sp", xav, s_x[:, :, col:col + n], reads=[B_sx], writes=[B_xall], sembuf=B_xall)
            rmsnorm_fm(xav, n, B_xall, lambda c, cd=cd: GM(layer, 0, c, cd), lambda c, cd=cd: MOD(layer, 0, c, cd),
                       lambda c, n=n: hgv[:, c, 0:n], B_hg, nt, psbank=7)
            nck = n // 128
            for ck0 in range(0, nck, 2):
                gens = [chunk_gen(RS[0], ck0)]
                if ck0 + 1 < nck:
                    gens.append(chunk_gen(RS[1], ck0 + 1))
                run_gens2(gens)
            for dc in range(8):
                sl = kk_ % 2
                kk_ += 1
                if kk_ == 1:
                    P.dma("pool", wso[0], d_sgop[0], writes=[B_wso[0]], sembuf=B_wso[0])
                P.dma("pool", wso[1 - sl], d_sgop[(dc + 1) % 8], writes=[B_wso[1 - sl]], sembuf=B_wso[1 - sl])
                pb = 6 + dc % 2
                for cc in range(16):
                    mm(PS[pb][:, 0:n], v3(wso[sl], 128)[:, cc, :], gTv[:, cc, 0:n], cc == 0, cc == 15, [B_wso[sl]] + B_gT, [PSB[pb]])
                stt(xav[:, dc, :], PS[pb][:, 0:n], MOD(layer, 2, dc, cd), xav[:, dc, :], ALU.mult, ALU.add,
                    [PSB[pb], B_mod[layer], B_xall], [B_xall])
            P.dma("sp", s_x[:, :, col:col + n], xav, reads=[B_xall], writes=[B_sx], sembuf=B_xall)
            col += n
        A.release(m0)
        P.barrier()


    def final_out():
        m0 = A.mark()
        NTK = 2 * LP + OUTS
        xg2 = [A.f32(8 * 512) for _ in range(2)]; Bxg2 = [Buf("xg_o0"), Buf("xg_o1")]
        yg = A.f32(8 * 512); Byg = Buf("yg_o"); ygv = v3(yg, 512)
        ot = [A.f32(D) for _ in range(2)]; Bot = [Buf("ot0"), Buf("ot1")]
        nt = norm_tmps(512)
        B_o = Buf("outs")
        col = 0
        k = 0
        npc = NTK // 512
        P.dma("sp", v3(xg2[0], 512), s_x[:, :, 0:512], reads=[B_sx], writes=[Bxg2[0]], sembuf=Bxg2[0])
        pi = 0
        while col < NTK:
            n = 512
            if pi + 1 < npc:
                P.dma("sp", v3(xg2[(pi + 1) % 2], 512), s_x[:, :, col + 512:col + 1024], reads=[B_sx], writes=[Bxg2[(pi + 1) % 2]],
                      sembuf=Bxg2[(pi + 1) % 2])
            xgv = v3(xg2[pi % 2], 512); Bxg = Bxg2[pi % 2]
            pi += 1
            rmsnorm_fm(xgv, n, Bxg, lambda c: V("norm_final", c), lambda c: None, lambda c: ygv[:, c, :], Byg, nt)
            for t4 in range(4):
                sl = k % 2
                k += 1
                for c in range(8):
                    pb = 4 + 2 * (c // 4) + (t4 % 2)
                    tr(PS[pb][:, (c % 4) * 128:(c % 4 + 1) * 128], ygv[:, c, t4 * 128:(t4 + 1) * 128], C("ident"), [Byg, B_cst], [PSB[pb]])
                cp("act", ot[sl][:, 0:512], PS[4 + (t4 % 2)][:, :], [PSB[4 + (t4 % 2)]], [Bot[sl]])
                cp("dve", ot[sl][:, 512:1024], PS[6 + (t4 % 2)][:, :], [PSB[6 + (t4 % 2)]], [Bot[sl]])
                tok = col + t4 * 128
                if tok < 2 * LP:
                    dst = o_yp[tok:tok + 128, :]
                else:
                    dst = o_ys[tok - 2 * LP:tok - 2 * LP + 128, :]
                P.dma("sp", dst, ot[sl], reads=[Bot[sl]], writes=[B_o], sembuf=Bot[sl])
            col += n
        A.release(m0)

    try:
        compute_mod(0)
        chk("mod0")
        phase0()
        chk("phase0")
        rwkv_layer()
        A.release(m_rwkv)
        P.barrier()
        chk("rwkv")
        compute_mod(1)
        ffn_layer(0, WIN, WIN - 128)
        chk("ffn0")
        sgu_layer(1, WIN - 128)
        chk("sgu")
        ffn_layer(1, WIN - 128, OUTS)
        chk("ffn1")
        final_out()
    except StopBuild:
        pass
    P.barrier()
    P.final_wait("sp")
    K.arena_hi = A.hi
    P.emit(ctx)
    ctx.close()
    return nc


def _fm(vec_1024):
    v = np.asarray(vec_1024, np.float32)
    lead = v.shape[:-1]
    n = v.shape[-1] // 128
    v = v.reshape(lead + (n, 128))
    return np.moveaxis(v, -1, 0)


def prep_inputs(x_prompt, x_sample, state_ctx_fwd, state_ctx_bwd, c, c_ctx,
           ada_w, ada_b, norm_mix, norm_ffn, ffn_up, ffn_conv, ffn_conv_b, ffn_down, norm_final,
           rw_mu, rw_wr, rw_wk, rw_wv, rw_wo, rw_w0, rw_w1, rw_w2, rw_a0, rw_a1, rw_a2,
           rw_g1, rw_g2, rw_kk, rw_ka, rw_rk, rw_lnx_w, rw_lnx_b,
           sg_in, sg_ln_w, sg_ln_b, sg_ws, sg_bs, sg_out):
    f = lambda a: np.ascontiguousarray(np.asarray(a, np.float32))
    x_prompt, x_sample = f(x_prompt), f(x_sample)
    consts = make_consts()
    def kcp(w):
        w = np.asarray(w, np.float32)
        return np.transpose(w.reshape(8, 128, w.shape[1]), (1, 0, 2))
    wr_, wk_, wv_ = (kcp(np.asarray(a)[0]) for a in (rw_wr, rw_wk, rw_wv))
    wkvp = np.stack([np.stack([w[:, :, hp * 128:(hp + 1) * 128] for w in (wr_, wk_, wv_)], 2) for hp in range(NHP)], 0)
    wkvp = f(wkvp.reshape(NHP, 128, 8 * 3 * 128))
    wo_ = np.transpose(np.asarray(rw_wo, np.float32)[0].reshape(8, 128, D), (1, 0, 2))
    wop = f(np.stack([wo_[:, :, dc * 128:(dc + 1) * 128] for dc in range(8)], 0).reshape(8, 128, 8 * 128))
    upp = np.zeros((2, NFC, 128, 8, 2, 128), np.float32)
    dnp = np.zeros((2, 8, 128, NFC, 128), np.float32)
    for l in range(2):
        u = kcp(np.asarray(ffn_up)[l])
        dn = np.transpose(np.asarray(ffn_down, np.float32)[l].reshape(NFC, 128, D), (1, 0, 2))
        for j in range(NFC):
            upp[l, j, :, :, 0, :] = u[:, :, j * 128:(j + 1) * 128]
            upp[l, j, :, :, 1, :] = u[:, :, F + j * 128:F + (j + 1) * 128]
        for dc in range(8):
            dnp[l, dc] = dn[:, :, dc * 128:(dc + 1) * 128]
    upp = f(upp.reshape(2, NFC, 128, 8 * 2 * 128))
    dnp = f(dnp.reshape(2, 8, 128, NFC * 128))
    so_ = np.transpose(np.asarray(sg_out, np.float32)[0].reshape(16, 128, D), (1, 0, 2))
    sgop = f(np.stack([so_[:, :, dc * 128:(dc + 1) * 128] for dc in range(8)], 0).reshape(8, 128, 16 * 128))
    in_maps = []
    for core in range(8):
        odd = core % 2 == 1
        b = core // 2
        xp = x_prompt[2 * core:2 * core + 2]
        xs = x_sample[b]
        if odd:
            xp = xp[:, ::-1]
            xs = xs[::-1]
        sts = (state_ctx_bwd, state_ctx_fwd) if odd else (state_ctx_fwd, state_ctx_bwd)
        st = np.zeros((2, NHP, 128, 64), np.float32)
        for d in range(2):
            S = np.asarray(sts[d], np.float32)[b, 0]
            Ht = np.transpose(S, (0, 2, 1))
            st[d] = Ht.reshape(NHP, 128, 64)
        cond = np.zeros((128, 8, 2), np.float32)
        cond[:, :, 0] = _fm(c_ctx)
        cond[:, :, 1] = _fm(np.asarray(c)[b])
        vecs = np.zeros((128, NV), np.float32)

        def put(name, arr):
            o, n = VEC[name]
            vecs[:, o:o + n] = np.asarray(arr, np.float32).reshape(128, n)
        put("ada_b", _fm(np.asarray(ada_b).reshape(2, 6 * D)).reshape(128, 96))
        put("norm_mix", _fm(norm_mix).reshape(128, 16))
        put("norm_ffn", _fm(norm_ffn).reshape(128, 16))
        put("norm_final", _fm(norm_final))
        put("conv_b", _fm(ffn_conv_b).reshape(128, 88))
        cw = np.asarray(ffn_conv, np.float32)
        if odd:
            cw = cw[:, ::-1, ::-1]
        cw = cw.reshape(2, 9, 2 * F)
        cwf = _fm(cw)
        put("conv_w", np.transpose(cwf, (0, 1, 3, 2)).reshape(128, 2 * 44 * 9))
        mu = np.asarray(rw_mu, np.float32)[0]
        if odd:
            mu = mu[::-1]
        muf = _fm(mu)
        put("mu", np.transpose(muf, (0, 3, 1, 2)).reshape(128, 96))
        dsel = [1, 0] if odd else [0, 1]
        put("w0", _fm(np.asarray(rw_w0)[0][dsel]).reshape(128, 16))
        put("a0", _fm(np.asarray(rw_a0)[0][dsel]).reshape(128, 16))
        put("k_k", _fm(np.asarray(rw_kk)[0]))
        put("k_a", _fm(np.asarray(rw_ka)[0]))
        put("r_k", _fm(np.asarray(rw_rk)[0].reshape(D)))
        put("lnx_w", _fm(np.asarray(rw_lnx_w)[0]))
        put("lnx_b", _fm(np.asarray(rw_lnx_b)[0]))
        ws = np.asarray(sg_ws, np.float32)[0]
        bsv = np.asarray(sg_bs, np.float32)[0]
        if odd:
            ws = ws[:, ::-1, ::-1]
            bsv = bsv[:, ::-1]
        wsT = np.transpose(ws, (2, 0, 1)).reshape(128, 8 * 128)
        m = {
            "xp": f(xp.reshape(2 * LP, D)), "xs": f(xs), "st": st, "cond": f(cond.reshape(128, 16)),
            "vec": vecs, "cst": consts,
            "ada_w": f(ada_w), "ffn_upp": upp, "ffn_dnp": dnp, "wkvp": wkvp, "wop": wop,
            "w1": f(np.asarray(rw_w1)[0][dsel]), "w2": f(np.asarray(rw_w2)[0][dsel]),
            "a1": f(np.asarray(rw_a1)[0][dsel]), "a2": f(np.asarray(rw_a2)[0][dsel]),
            "g1": f(np.asarray(rw_g1)[0]), "g2": f(np.asarray(rw_g2)[0]),
            "sg_in": f(np.asarray(sg_in)[0]), "sg_outp": sgop,
            "sg_lnw": f(np.broadcast_to(np.asarray(sg_ln_w, np.float32)[0][None, :], (128, E2))),
            "sg_lnb": f(np.broadcast_to(np.asarray(sg_ln_b, np.float32)[0][None, :], (128, E2))),
            "sg_bs": f(bsv.T), "sg_wsT": f(wsT),
        }
        in_maps.append(m)
    return in_maps


def assemble(results):
    y_prompt = np.zeros((16, LP, D), np.float32)
    y_sample = np.zeros((4, LS, D), np.float32)
    nsf = np.zeros((16, 1, 16, 64, 64), np.float32)
    nsb = np.zeros((16, 1, 16, 64, 64), np.float32)
    for core in range(8):
        r = results[core]
        odd = core % 2 == 1
        b = core // 2
        yp = np.asarray(r["yp"], np.float32).reshape(2, LP, D)
        ys = np.asarray(r["ys"], np.float32)
        ns = np.asarray(r["ns"], np.float32)
        if odd:
            yp = yp[:, ::-1]
            y_sample[b, LS - OUTS:] = ys[::-1]
        else:
            y_sample[b, :OUTS] = ys
        y_prompt[2 * core:2 * core + 2] = yp
        for si in range(2):
            for d in range(2):
                S = np.transpose(ns[si, d].reshape(16, 64, 64), (0, 2, 1))
                tgt_fwd = (d == 0) != odd
                if tgt_fwd:
                    nsf[2 * core + si, 0] = S
                else:
                    nsb[2 * core + si, 0] = S
    return (y_prompt, y_sample, nsf, nsb)


def kernel(**inputs):
    in_maps = prep_inputs(**inputs)
    nc = build_program()
    res = run_bass_kernel_spmd(nc, in_maps, core_ids=list(range(8)))
    return assemble(res.results)
```
